# Optimizing a Trainium2 kernel written in Bass

```python
import jax, jax.numpy as jnp
from jax import lax
import numpy as np

D_MODEL = 1024
BATCH = 8
SEQ = 4096
DEPTH = 4

CHUNK = 64
Q_BLOCK = 128

GDN_HEADS = 4
GDN_HEAD_DIM = 128
GDN_WIDTH = GDN_HEADS * GDN_HEAD_DIM
GDN_CONV = 4
SB_HEADS = 8
SB_HEAD_DIM = 64
SB_WIDTH = SB_HEADS * SB_HEAD_DIM
SC_GROUPS = 8
SC_WIDTH = 512
SC_CONV = 3
N_BRANCH = 3
BRANCH_WIDTH = 512
D_FF = 4 * D_MODEL
EPS = 1e-6

_SIZES = (3 * GDN_WIDTH,
          GDN_WIDTH,
          GDN_HEADS,
          GDN_HEADS,
          3 * SB_WIDTH,
          SC_WIDTH,
          SC_WIDTH,
          SC_WIDTH,
          N_BRANCH * D_MODEL)
IN_PROJ_WIDTH = sum(_SIZES)
SPLIT_POINTS = tuple(int(s) for s in np.cumsum(_SIZES)[:-1])

kernel_name = "hybrid_gdn_stickbreak_shortconv_block"


def rms_norm(x, w):
    x32 = x.astype(jnp.float32)
    y = x32 * lax.rsqrt(jnp.mean(x32 * x32, axis=-1, keepdims=True) + EPS)
    return (y * w.astype(jnp.float32)).astype(x.dtype)


def l2_normalize(x):
    return x * lax.rsqrt(jnp.sum(x * x, axis=-1, keepdims=True) + EPS)


def causal_depthwise_conv(x, w):
    k_len, ch = w.shape
    return lax.conv_general_dilated(
        x, w.astype(x.dtype)[:, None, :], window_strides=(1,), padding=[(k_len - 1, 0)],
        dimension_numbers=("NWC", "WIO", "NWC"), feature_group_count=ch)


def gated_delta_rule(q, k, v, g, beta):
    bsz, seq, heads, dk = q.shape
    dv = v.shape[-1]
    n_chunks = seq // CHUNK

    def chunkify(t):
        return jnp.moveaxis(t.reshape((bsz, n_chunks, CHUNK) + t.shape[2:]), 3, 2)

    q, k, v, g, beta = (chunkify(t) for t in (q, k, v, g, beta))
    g = jnp.cumsum(g, axis=-1)
    tri = jnp.tril(jnp.ones((CHUNK, CHUNK), dtype=bool))
    strict = jnp.tril(jnp.ones((CHUNK, CHUNK), dtype=bool), k=-1)
    decay = jnp.exp(jnp.where(tri, g[..., :, None] - g[..., None, :], -jnp.inf))

    k_beta = k * beta[..., None]
    v_beta = v * beta[..., None]
    a_kk = jnp.where(strict, jnp.einsum("bnhcd,bnhed->bnhce", k_beta, k) * decay, 0.0)
    lhs = a_kk + jnp.eye(CHUNK, dtype=a_kk.dtype)
    rhs = jnp.concatenate([v_beta, k_beta * jnp.exp(g)[..., None]], axis=-1)
    sol = lax.linalg.triangular_solve(lhs, rhs, left_side=True, lower=True, unit_diagonal=True)
    u, w = sol[..., :dv], sol[..., dv:]

    a_qk = jnp.where(tri, jnp.einsum("bnhcd,bnhed->bnhce", q, k) * decay, 0.0)
    q_dec = q * jnp.exp(g)[..., None]
    k_dec = k * jnp.exp(g[..., -1:] - g)[..., None]
    g_last = jnp.exp(g[..., -1])

    def step(state, inp):
        q_c, k_c, u_c, w_c, a_c, gl_c = inp
        v_new = u_c - jnp.einsum("bhcd,bhde->bhce", w_c, state)
        o_c = (jnp.einsum("bhcd,bhde->bhce", q_c, state)
               + jnp.einsum("bhcs,bhse->bhce", a_c, v_new))
        state = state * gl_c[..., None, None] + jnp.einsum("bhcd,bhce->bhde", k_c, v_new)
        return state, o_c

    xs = tuple(jnp.moveaxis(t, 1, 0) for t in (q_dec, k_dec, u, w, a_qk, g_last))
    state0 = jnp.zeros((bsz, heads, dk, dv), jnp.float32)
    _, o = lax.scan(step, state0, xs)
    return o.transpose(1, 0, 3, 2, 4).reshape(bsz, seq, heads, dv)


def gdn_branch(qkv, gate, a, b, conv_w, a_log, dt_bias, norm_w):
    bsz, seq, _ = qkv.shape
    out_dtype = qkv.dtype
    qkv = jax.nn.silu(causal_depthwise_conv(qkv, conv_w)).astype(jnp.float32)
    q, k, v = jnp.split(qkv, 3, axis=-1)
    shp = (bsz, seq, GDN_HEADS, GDN_HEAD_DIM)
    q = l2_normalize(q.reshape(shp)) * (GDN_HEAD_DIM ** -0.5)
    k = l2_normalize(k.reshape(shp))
    v = v.reshape(shp)
    beta = jax.nn.sigmoid(b.astype(jnp.float32))
    g = -jnp.exp(a_log.astype(jnp.float32)) * jax.nn.softplus(a.astype(jnp.float32) + dt_bias.astype(jnp.float32))
    o = gated_delta_rule(q, k, v, g, beta)
    o = o * lax.rsqrt(jnp.mean(o * o, axis=-1, keepdims=True) + EPS) * norm_w.astype(jnp.float32)
    o = o * jax.nn.silu(gate.astype(jnp.float32).reshape(shp))
    return o.reshape(bsz, seq, GDN_WIDTH).astype(out_dtype)


def stick_breaking_branch(qkv):
    bsz, seq, _ = qkv.shape
    q, k, v = jnp.split(qkv, 3, axis=-1)
    shp = (bsz, seq, SB_HEADS, SB_HEAD_DIM)
    q, k, v = q.reshape(shp), k.reshape(shp), v.reshape(shp)
    scale = SB_HEAD_DIM ** -0.5
    outs = []
    for blk in range(seq // Q_BLOCK):
        q0 = blk * Q_BLOCK
        k_len = q0 + Q_BLOCK
        z = jnp.einsum("bqhd,bkhd->bhqk", q[:, q0:k_len], k[:, :k_len]).astype(jnp.float32) * scale
        t_idx = q0 + jnp.arange(Q_BLOCK)
        s_idx = jnp.arange(k_len)
        mask = s_idx[None, :] < t_idx[:, None]
        log_1m = jnp.where(mask, jax.nn.log_sigmoid(-z), 0.0)
        after = lax.cumsum(log_1m, axis=3, reverse=True) - log_1m
        att = jnp.where(mask, jnp.exp(jax.nn.log_sigmoid(z) + after), 0.0)
        outs.append(jnp.einsum("bhqk,bkhd->bqhd", att.astype(v.dtype), v[:, :k_len]))
    return jnp.concatenate(outs, axis=1).reshape(bsz, seq, SB_WIDTH)


def short_conv_branch(xin, gate_b, gate_c, conv_w):
    return gate_b * causal_depthwise_conv(gate_c * xin, conv_w)


def setup_inputs(seed: int = 0) -> dict:
    key = jax.random.key(seed)
    ks = jax.random.split(key, 16)
    f32 = jnp.float32

    def normal(k, shape, scale):
        return jax.random.normal(k, shape, f32) * scale

    def gain(k):
        return 1.0 + 0.02 * jax.random.normal(k, (DEPTH, D_MODEL), f32)

    dt = jnp.exp(jax.random.uniform(ks[5], (DEPTH, GDN_HEADS), f32, np.log(1e-3), np.log(1e-1)))
    return {
        "x": jax.random.normal(ks[0], (BATCH, SEQ, D_MODEL), f32),
        "norm_mix_pre": gain(ks[1]),
        "w_in": normal(ks[2], (DEPTH, D_MODEL, IN_PROJ_WIDTH), D_MODEL ** -0.5),
        "conv_qkv_w": normal(ks[3], (DEPTH, GDN_CONV, 3 * GDN_WIDTH), GDN_CONV ** -0.5),
        "gdn_a_log": jnp.log(jax.random.uniform(ks[4], (DEPTH, GDN_HEADS), f32, 1.0, 16.0)),
        "gdn_dt_bias": dt + jnp.log(-jnp.expm1(-dt)),
        "gdn_norm_w": 1.0 + 0.02 * jax.random.normal(ks[6], (DEPTH, GDN_HEAD_DIM), f32),
        "conv_sc_w": normal(ks[7], (DEPTH, SC_CONV, SC_WIDTH), SC_CONV ** -0.5),
        "w_branch": normal(ks[8], (DEPTH, N_BRANCH, BRANCH_WIDTH, D_MODEL), BRANCH_WIDTH ** -0.5),
        "w_out": normal(ks[9], (DEPTH, D_MODEL, D_MODEL), D_MODEL ** -0.5),
        "norm_mix_post": gain(ks[10]),
        "norm_ffn_pre": gain(ks[11]),
        "w_ff1": normal(ks[12], (DEPTH, D_MODEL, D_FF), D_MODEL ** -0.5),
        "w_ff2": normal(ks[13], (DEPTH, D_FF, D_MODEL), D_FF ** -0.5),
        "norm_ffn_post": gain(ks[14]),
    }


def reference(x, norm_mix_pre, w_in, conv_qkv_w, gdn_a_log, gdn_dt_bias, gdn_norm_w, conv_sc_w,
              w_branch, w_out, norm_mix_post, norm_ffn_pre, w_ff1, w_ff2, norm_ffn_post):
    bsz, seq, _ = x.shape
    for l in range(DEPTH):
        h = rms_norm(x, norm_mix_pre[l])
        proj = h @ w_in[l]
        (gdn_qkv, gdn_gate, gdn_a, gdn_b, sb_qkv, sc_x, sc_b, sc_c, gates) = jnp.split(
            proj, SPLIT_POINTS, axis=-1)
        y_a = gdn_branch(gdn_qkv, gdn_gate, gdn_a, gdn_b, conv_qkv_w[l], gdn_a_log[l],
                         gdn_dt_bias[l], gdn_norm_w[l])
        y_b = stick_breaking_branch(sb_qkv)
        y_c = short_conv_branch(sc_x, sc_b, sc_c, conv_sc_w[l])
        gates = jax.nn.sigmoid(gates.reshape(bsz, seq, N_BRANCH, D_MODEL))
        merged = (gates[:, :, 0] * (y_a @ w_branch[l, 0])
                  + gates[:, :, 1] * (y_b @ w_branch[l, 1])
                  + gates[:, :, 2] * (y_c @ w_branch[l, 2]))
        x = x + rms_norm(merged @ w_out[l], norm_mix_post[l])
        h = rms_norm(x, norm_ffn_pre[l])
        f = jnp.square(jax.nn.relu(h @ w_ff1[l])) @ w_ff2[l]
        x = x + rms_norm(f, norm_ffn_post[l])
    return x
```

```python
import numpy as np
import concourse.bass as bass
import concourse.mybir as mybir
from concourse.bass_utils import run_bass_kernel_spmd

F32 = mybir.dt.float32
BF16 = mybir.dt.bfloat16
AF = mybir.ActivationFunctionType
ALU = mybir.AluOpType
EPS = 1e-6
B_STOP = 0
DBL_H = 0
DBL_L = 5
DBL_BAR = 0
DBL_PARTS = 4


class Cfg:
    def __init__(self, S=4096, D=1024, HG=4, HS=8, DFF=4096, NL=4, TB=512):
        self.S, self.D, self.HG, self.HS, self.DFF, self.NL = S, D, HG, HS, DFF, NL
        self.BW = 128 * HG
        assert self.BW == 64 * HS
        self.KC = D // 128
        self.BC = self.BW // 128
        self.FC = DFF // 128
        self.TB = min(TB, S)
        self.NTB = S // self.TB
        self.NT = S // 128
        BW = self.BW
        self.o_gq, self.o_gate = 0, 3 * BW
        self.o_a, self.o_b = 4 * BW, 4 * BW + HG
        self.o_sq = 4 * BW + 2 * HG
        self.o_scx = self.o_sq + 3 * BW
        self.o_gates = self.o_scx + 3 * BW
        self.NIN = self.o_gates + 3 * D
        KC, BC = self.KC, self.BC
        c = 0
        self.p_n1 = c; c += KC
        self.p_n2 = c; c += KC
        self.p_n3 = c; c += KC
        self.p_n4 = c; c += KC
        self.p_cq = c; c += 4 * 3 * BC
        self.p_cs = c; c += 3 * BC
        self.p_gn = c; c += 1
        self.p_al = c; c += HG
        self.p_dt = c; c += HG
        self.NP = c


class Tok:
    __slots__ = ("w", "r")

    def __init__(self):
        self.w = None
        self.r = []


class FW:
    def __init__(self, nc, nds=40):
        self.nc = nc
        self.eng = {"pe": nc.tensor, "act": nc.scalar, "dve": nc.vector, "pool": nc.gpsimd, "sp": nc.sync}
        self.sem = {e: nc.alloc_semaphore(name="prog_" + e) for e in self.eng}
        self.cnt = {e: 0 for e in self.eng}
        self.seen = {e: {} for e in self.eng}
        self.dsem = [nc.alloc_semaphore(name="dsem%d" % i) for i in range(nds)]
        self.dval = [0] * nds
        self.drr = 0
        self.nwait = 0
        self.nins = 0
        self.nrot = 0
        self.old = []

    def _wait(self, e, ev, kind):
        if ev is None:
            return
        sem, val, src = ev
        if src == e and (e == "pe" or kind == "war"):
            return
        key = sem.num
        if self.seen[e].get(key, 0) >= val:
            return
        self.eng[e].wait_ge(sem, val)
        self.seen[e][key] = val
        self.nwait += 1

    def _deps(self, e, reads, writes):
        for t in reads:
            self._wait(e, t.w, "raw")
        for t in writes:
            self._wait(e, t.w, "waw")
            for ev in t.r:
                self._wait(e, ev, "war")

    def _mark(self, ev, reads, writes):
        for t in reads:
            t.r = [x for x in t.r if x[0].num != ev[0].num] + [ev]
        for t in writes:
            t.w = ev
            t.r = []

    def op(self, e, fn, reads=(), writes=()):
        self._deps(e, reads, writes)
        ins = fn(self.eng[e])
        if self.cnt[e] >= 30000:
            self.nrot += 1
            self.old.append((self.sem[e], self.cnt[e], e))
            self.sem[e] = self.nc.alloc_semaphore(name="prog_%s_%d" % (e, self.nrot))
            self.cnt[e] = 0
        self.cnt[e] += 1
        ins.then_inc(self.sem[e], 1)
        ev = (self.sem[e], self.cnt[e], e)
        self._mark(ev, reads, writes)
        self.nins += 1
        return ev

    def dma(self, q, out, in_, reads=(), writes=()):
        self._deps(q, reads, writes)
        i = self.drr
        self.drr = (self.drr + 1) % len(self.dsem)
        self._wait(q, (self.dsem[i], self.dval[i], "dma"), "raw")
        self.dval[i] += 16
        self.eng[q].dma_start(out=out, in_=in_).then_inc(self.dsem[i], 16)
        ev = (self.dsem[i], self.dval[i], "dma")
        self._mark(ev, reads, writes)
        self.nins += 1
        return ev

    def barrier(self, engines=("pe", "act", "dve", "pool", "sp")):
        for e in engines:
            for ev in self.old:
                if ev[2] != e:
                    self._wait(e, ev, "raw")
            for f in self.eng:
                if f != e and self.cnt[f] > 0:
                    self._wait(e, (self.sem[f], self.cnt[f], f), "raw")
            for i in range(len(self.dsem)):
                if self.dval[i] > 0:
                    self._wait(e, (self.dsem[i], self.dval[i], "dma"), "raw")


class Ring:
    def __init__(self, alloc, name, n, shape, dt):
        self.bufs = []
        for i in range(n):
            self.bufs.append((alloc("%s_%d" % (name, i), shape, dt), Tok()))
        self.i = 0

    def next(self):
        b = self.bufs[self.i]
        self.i = (self.i + 1) % len(self.bufs)
        return b


def act(fw, out, in_, func, reads, writes, **kw):
    return fw.op("act", lambda e: e.activation(out=out, in_=in_, func=func, **kw), reads, writes)


def build(cfg, phases=("A",), dump=False):
    C = cfg
    nc = bass.Bass("TRN2", target_bir_lowering=False)
    S, D, KC, BW, BC, TB, NTB, NT, HG, HS = C.S, C.D, C.KC, C.BW, C.BC, C.TB, C.NTB, C.NT, C.HG, C.HS
    xin = nc.dram_tensor("xT", [D, S], F32, kind="ExternalInput").ap()
    w_in = nc.dram_tensor("w_in", [C.NL, D, C.NIN], F32, kind="ExternalInput").ap()
    w_br = nc.dram_tensor("w_branch", [C.NL, 3, BW, D], F32, kind="ExternalInput").ap()
    w_out = nc.dram_tensor("w_out", [C.NL, D, D], F32, kind="ExternalInput").ap()
    w_ff1 = nc.dram_tensor("w_ff1", [C.NL, D, C.DFF], F32, kind="ExternalInput").ap()
    w_ff2 = nc.dram_tensor("w_ff2", [C.NL, C.DFF, D], F32, kind="ExternalInput").ap()
    prm = nc.dram_tensor("prm", [C.NL, 128, C.NP], F32, kind="ExternalInput").ap()
    yout = nc.dram_tensor("yT", [D, S], F32, kind="ExternalOutput").ap()
    okind = "ExternalOutput" if dump else "Internal"
    NFM = 4 * BW + 3 * BW + 3 * D
    projT = nc.dram_tensor("projT", [NFM, S], F32, kind=okind).ap()
    r_gq, r_gate, r_sc, r_gates = 0, 3 * BW, 4 * BW, 7 * BW
    sqkT = nc.dram_tensor("sqkT", [2 * BW, S], BF16, kind=okind).ap()
    sv_tm = nc.dram_tensor("sv_tm", [S, BW], BF16, kind=okind).ap()
    ab_tm = nc.dram_tensor("ab_tm", [S, 2 * HG], F32, kind=okind).ap()

    fw = FW(nc)
    import contextlib
    st = {"es": None, "n": 0}

    def sb(name, shape, dt):
        st["n"] += 1
        return st["es"].enter_context(nc.sbuf_tensor("%s_%d" % (name, st["n"]), shape, dt))

    def psa(name, shape, dt):
        return nc.alloc_psum_tensor(name, shape, dt)

    def RingS(name, n, shape, dt):
        return Ring(sb, name, n, shape, dt)

    ones_f = nc.alloc_sbuf_tensor("ones_f", [128, 128], F32)
    t_const = Tok()
    fw.op("dve", lambda e: e.memset(ones_f[:], 1.0), [], [t_const])
    prm_sb = nc.alloc_sbuf_tensor("prm_sb", [128, C.NL, C.NP], F32)
    t_prm = Tok()
    for l in range(C.NL):
        fw.dma("sp", prm_sb[:, l, :], prm[l], [], [t_prm])
    PS = Ring(psa, "ps", 4, [128, 512], F32)
    PD7 = (psa("pd7", [128, 512], F32), Tok())
    PD8 = (psa("pd8", [128, 512], F32), Tok())
    PSO = Ring(psa, "pso", 2, [128, 512], F32)

    def phase_A(l, xsrc):
        h = sb("A_h", [128, KC, S], BF16)
        t_h = [Tok() for _ in range(NTB)]
        xs = RingS("A_xs", 2, [128, KC, TB], F32)
        sqt = RingS("A_sq", 2, [128, TB], F32)
        rs = RingS("A_rs", 2, [128, TB], F32)
        xv = xsrc.rearrange("(c p) s -> p c s", p=128)
        for tb in range(NTB):
            t0 = tb * TB
            xt, t_x = xs.next()
            fw.dma("sp", xt[:], xv[:, :, t0:t0 + TB], [], [t_x])
            ps, t_ps = PS.next()
            for kc in range(KC):
                sq, t_sq = sqt.next()
                act(fw, sq[:], xt[:, kc, :], AF.Square, [t_x], [t_sq])
                fw.op("pe", lambda e: e.matmul(ps[:, 0:TB], ones_f[:], sq[:], start=(kc == 0), stop=(kc == KC - 1)),
                      [t_sq, t_const], [t_ps])
            r, t_r = rs.next()
            act(fw, r[:], ps[:, 0:TB], AF.Sqrt, [t_ps], [t_r], scale=1.0 / D, bias=EPS)
            fw.op("dve", lambda e: e.reciprocal(out=r[:], in_=r[:]), [t_r], [t_r])
            for kc in range(KC):
                fw.op("dve", lambda e: e.scalar_tensor_tensor(
                    out=h[:, kc, t0:t0 + TB], in0=xt[:, kc, :], scalar=prm_sb[:, l, C.p_n1 + kc:C.p_n1 + kc + 1],
                    in1=r[:], op0=ALU.mult, op1=ALU.mult), [t_x, t_r, t_prm], [t_h[tb]])
        wst = RingS("A_wst", 3, [128, KC, 128], F32)
        wbf = RingS("A_wbf", 3, [128, KC, 128], BF16)
        NH2 = 2 if NTB >= 2 else 1
        SH = S // NH2
        ostf = RingS("A_of", 2, [128, SH], F32)
        ostb = RingS("A_ob", 2, [128, SH], BF16)
        wv = w_in[l].rearrange("(c p) n -> p c n", p=128)
        segs = []
        for i in range(4 * BC):
            segs.append((C.o_gq + 128 * i, projT[r_gq + 128 * i:r_gq + 128 * (i + 1), :], False))
        for i in range(2 * BC):
            segs.append((C.o_sq + 128 * i, sqkT[128 * i:128 * (i + 1), :], True))
        for i in range(3 * BC):
            segs.append((C.o_scx + 128 * i, projT[r_sc + 128 * i:r_sc + 128 * (i + 1), :], False))
        for i in range(3 * KC):
            segs.append((C.o_gates + 128 * i, projT[r_gates + 128 * i:r_gates + 128 * (i + 1), :], False))
        nev = 0
        for (wc, dst, isb) in segs:
            ws, t_ws = wst.next()
            fw.dma("sp", ws[:], wv[:, :, wc:wc + 128], [], [t_ws])
            wb, t_wb = wbf.next()
            fw.op("pool", lambda e: e.tensor_copy(out=wb[:], in_=ws[:]), [t_ws], [t_wb])
            for hf in range(NH2):
                ost, t_o = (ostb if isb else ostf).next()
                for tb in range(hf * NTB // NH2, (hf + 1) * NTB // NH2):
                    t0 = tb * TB
                    o0 = t0 - hf * SH
                    ps, t_ps = PS.next()
                    for kc in range(KC):
                        fw.op("pe", lambda e: e.matmul(ps[:, 0:TB], wb[:, kc, :], h[:, kc, t0:t0 + TB],
                                                       start=(kc == 0), stop=(kc == KC - 1)), [t_wb, t_h[tb]], [t_ps])
                    if nev % 2 == 0:
                        act(fw, ost[:, o0:o0 + TB], ps[:, 0:TB], AF.Copy, [t_ps], [t_o])
                    else:
                        fw.op("dve", lambda e: e.tensor_copy(out=ost[:, o0:o0 + TB], in_=ps[:, 0:TB]), [t_ps], [t_o])
                    nev += 1
                fw.dma("sp", dst[:, hf * SH:(hf + 1) * SH], ost[:], [t_o], [])
        wvs = sb("A_wvs", [128, KC, BW + 2 * HG], F32)
        wvb = sb("A_wvb", [128, KC, BW + 2 * HG], BF16)
        t_wv = Tok()
        fw.dma("sp", wvs[:, :, 0:BW], wv[:, :, C.o_sq + 2 * BW:C.o_sq + 3 * BW], [], [t_wv])
        fw.dma("sp", wvs[:, :, BW:BW + 2 * HG], wv[:, :, C.o_a:C.o_a + 2 * HG], [], [t_wv])
        t_wvb = Tok()
        fw.op("pool", lambda e: e.tensor_copy(out=wvb[:], in_=wvs[:]), [t_wv], [t_wvb])
        vst = RingS("A_vst", 2, [128, 4, BW], BF16)
        abst = sb("A_abst", [128, NT, 2 * HG], F32)
        t_ab = Tok()
        svv = sv_tm.rearrange("(t p) n -> p t n", p=128)
        for tg in range(NT // 4 if NT >= 4 else 1):
            ntile = min(4, NT)
            vs, t_vs = vst.next()
            for j in range(ntile):
                tt = tg * 4 + j
                tbi = (tt * 128) // TB
                ps, t_ps = PS.next()
                for kc in range(KC):
                    fw.op("pe", lambda e: e.matmul(ps[:, 0:BW], h[:, kc, tt * 128:(tt + 1) * 128], wvb[:, kc, 0:BW],
                                                   start=(kc == 0), stop=(kc == KC - 1)), [t_wvb, t_h[tbi]], [t_ps])
                act(fw, vs[:, j, :], ps[:, 0:BW], AF.Copy, [t_ps], [t_vs])
                ps2, t_ps2 = PS.next()
                for kc in range(KC):
                    fw.op("pe", lambda e: e.matmul(ps2[:, 0:2 * HG], h[:, kc, tt * 128:(tt + 1) * 128],
                                                   wvb[:, kc, BW:BW + 2 * HG],
                                                   start=(kc == 0), stop=(kc == KC - 1)), [t_wvb, t_h[tbi]], [t_ps2])
                fw.op("dve", lambda e: e.tensor_copy(out=abst[:, tt, :], in_=ps2[:, 0:2 * HG]), [t_ps2], [t_ab])
            fw.dma("sp", svv[:, tg * 4:tg * 4 + ntile, :], vs[:, 0:ntile, :], [t_vs], [])
        abv_ = ab_tm.rearrange("(t p) n -> p t n", p=128)
        for g_ in range(0, NT, 8):
            fw.dma("sp", abv_[:, g_:min(NT, g_ + 8), :], abst[:, g_:min(NT, g_ + 8), :], [t_ab], [])

    TQ = min(512, S)
    NQB = S // TQ
    MD = TQ // 128
    ybT = nc.dram_tensor("ybT", [BW, S], BF16, kind=okind).ap()
    ycT = nc.dram_tensor("ycT", [BW, S], BF16, kind=okind).ap()
    negL8 = nc.alloc_sbuf_tensor("negL8", [128, 128], BF16)
    ones8 = nc.alloc_sbuf_tensor("ones8", [128, 128], BF16)
    maskD = nc.alloc_sbuf_tensor("maskD", [128, MD, TQ], BF16)
    ctmp = nc.alloc_sbuf_tensor("ctmp", [128, TQ], BF16)
    fw.op("pool", lambda e: e.memset(ctmp[:], -8.0), [], [t_const])
    fw.op("pool", lambda e: e.affine_select(out=negL8[:], in_=ctmp[:, 0:128], pattern=[[-1, 128]],
                                            compare_op=ALU.is_ge, fill=0.0, base=0, channel_multiplier=1),
          [t_const], [t_const])
    fw.op("pool", lambda e: e.memset(ones8[:], 8.0), [], [t_const])
    fw.op("pool", lambda e: e.memset(ctmp[:], 1.0), [t_const], [t_const])
    for m in range(MD):
        fw.op("pool", lambda e: e.affine_select(out=maskD[:, m, :], in_=ctmp[:], pattern=[[1, TQ]],
                                                compare_op=ALU.is_gt, fill=0.0, base=-128 * m, channel_multiplier=-1),
              [t_const], [t_const])

    def phase_C(l):
        qk = sb("C_qk", [128, 2 * BC, S], BF16)
        vv = sb("C_v", [128, NT, BW], BF16)
        yb = sb("C_yb", [128, BC, S], BF16)
        t_qk, t_v, t_yb = Tok(), Tok(), Tok()
        for c_ in range(2 * BC):
            fw.dma("sp", qk[:, c_, :], sqkT[c_ * 128:(c_ + 1) * 128, :], [], [t_qk])
        svv_ = sv_tm.rearrange("(t p) n -> p t n", p=128)
        for g_ in range(0, NT, 4):
            fw.dma("sp", vv[:, g_:min(NT, g_ + 4), :], svv_[:, g_:min(NT, g_ + 4), :], [], [t_v])
        e_r = RingS("C_e", 2, [128, TQ], F32)
        sp_r = RingS("C_sp", 3, [128, TQ], BF16)
        ar_r = RingS("C_ar", 2, [128, TQ], F32)
        at_r = RingS("C_at", 3, [128, TQ], BF16)
        rs_r = RingS("C_rs", 2, [128, TQ], F32)
        for I in range(NQB):
            q0 = I * TQ
            nkb = (q0 + TQ) // 128
            for hh in range(HS):
                hp, hc = hh % 2, hh // 2
                pl, ph = hp * 64, hp * 64 + 64
                po, t_po = PSO.next()
                rsb, t_rs = rs_r.next()
                qT = qk[pl:ph, hc, q0:q0 + TQ]
                for j in range(nkb - 1, -1, -1):
                    diag = (128 * j >= q0)
                    m = (128 * j - q0) // 128
                    kT = qk[pl:ph, BC + hc, 128 * j:128 * (j + 1)]
                    pz, t_pz = PS.next()
                    fw.op("pe", lambda e: e.matmul(pz[:, 0:TQ], kT, qT, start=True, stop=True), [t_qk], [t_pz])
                    eb, t_e = e_r.next()
                    act(fw, eb[:], pz[:, 0:TQ], AF.Exp, [t_pz], [t_e], scale=0.125)
                    spb, t_sp = sp_r.next()
                    act(fw, spb[:], eb[:], AF.Ln, [t_e], [t_sp], bias=1.0)
                    if diag:
                        fw.op("pool", lambda e: e.tensor_tensor(out=spb[:], in0=spb[:], in1=maskD[:, m, :], op=ALU.mult),
                              [t_sp, t_const], [t_sp])
                    pa, t_pa = PS.next()
                    fw.op("pe", lambda e: e.matmul(pa[:, 0:TQ], kT, qT, start=True, stop=False), [t_qk], [t_pa])
                    fw.op("pe", lambda e: e.matmul(pa[:, 0:TQ], negL8[:], spb[:], start=False, stop=True),
                          [t_sp, t_const], [t_pa])
                    atb, t_at = at_r.next()
                    if j == nkb - 1:
                        act(fw, atb[:], pa[:, 0:TQ], AF.Exp, [t_pa], [t_at], scale=0.125)
                    else:
                        arb, t_ar = ar_r.next()
                        fw.op("dve", lambda e: e.tensor_tensor(out=arb[:], in0=pa[:, 0:TQ], in1=rsb[:], op=ALU.subtract),
                              [t_pa, t_rs], [t_ar])
                        act(fw, atb[:], arb[:], AF.Exp, [t_ar], [t_at], scale=0.125)
                    if diag:
                        fw.op("pool", lambda e: e.tensor_tensor(out=atb[:], in0=atb[:], in1=maskD[:, m, :], op=ALU.mult),
                              [t_at, t_const], [t_at])
                    fw.op("pe", lambda e: e.matmul(po[pl:ph, 0:TQ], vv[:, j, hh * 64:(hh + 1) * 64], atb[:],
                                                   start=(j == nkb - 1), stop=(j == 0)), [t_v, t_at], [t_po])
                    if j > 0:
                        pc, t_pc = PS.next()
                        fw.op("pe", lambda e: e.matmul(pc[:, 0:TQ], ones8[:], spb[:], start=True, stop=True),
                              [t_sp, t_const], [t_pc])
                        if j == nkb - 1:
                            fw.op("dve", lambda e: e.tensor_copy(out=rsb[:], in_=pc[:, 0:TQ]), [t_pc], [t_rs])
                        else:
                            fw.op("dve", lambda e: e.tensor_tensor(out=rsb[:], in0=pc[:, 0:TQ], in1=rsb[:], op=ALU.add),
                                  [t_pc, t_rs], [t_rs])
                act(fw, yb[pl:ph, hc, q0:q0 + TQ], po[pl:ph, 0:TQ], AF.Copy, [t_po], [t_yb])
        for c_ in range(BC):
            fw.dma("sp", ybT[c_ * 128:(c_ + 1) * 128, :], yb[:, c_, :], [t_yb], [])

    def phase_D(l):
        xr = RingS("D_x", 2, [128, TB + 2], F32)
        cr = RingS("D_c", 2, [128, TB + 2], F32)
        br = RingS("D_b", 2, [128, TB], F32)
        ar = RingS("D_a", 2, [128, TB], F32)
        yr = RingS("D_y", 2, [128, TB], BF16)
        for c in range(BC):
            rx, rb, rc = r_sc + 128 * c, r_sc + BW + 128 * c, r_sc + 2 * BW + 128 * c
            for tb in range(NTB):
                t0 = tb * TB
                xt, t_x = xr.next()
                ct, t_c = cr.next()
                bt, t_b = br.next()
                if tb == 0:
                    fw.op("dve", lambda e: e.memset(xt[:, 0:2], 0.0), [], [t_x])
                    fw.op("dve", lambda e: e.memset(ct[:, 0:2], 0.0), [], [t_c])
                    fw.dma("sp", xt[:, 2:TB + 2], projT[rx:rx + 128, 0:TB], [], [t_x])
                    fw.dma("sp", ct[:, 2:TB + 2], projT[rc:rc + 128, 0:TB], [], [t_c])
                else:
                    fw.dma("sp", xt[:], projT[rx:rx + 128, t0 - 2:t0 + TB], [], [t_x])
                    fw.dma("sp", ct[:], projT[rc:rc + 128, t0 - 2:t0 + TB], [], [t_c])
                fw.dma("sp", bt[:], projT[rb:rb + 128, t0:t0 + TB], [], [t_b])
                fw.op("dve", lambda e: e.tensor_tensor(out=xt[:], in0=xt[:], in1=ct[:], op=ALU.mult), [t_x, t_c], [t_x])
                at, t_a = ar.next()
                wcol = lambda tap: prm_sb[:, l, C.p_cs + tap * BC + c:C.p_cs + tap * BC + c + 1]
                fw.op("dve", lambda e: e.tensor_scalar(out=at[:], in0=xt[:, 0:TB], scalar1=wcol(0), scalar2=None,
                                                       op0=ALU.mult), [t_x, t_prm], [t_a])
                for tap in (1, 2):
                    fw.op("dve", lambda e: e.scalar_tensor_tensor(out=at[:], in0=xt[:, tap:tap + TB], scalar=wcol(tap),
                                                                  in1=at[:], op0=ALU.mult, op1=ALU.add),
                          [t_x, t_prm, t_a], [t_a])
                yt, t_y = yr.next()
                fw.op("dve", lambda e: e.tensor_tensor(out=yt[:], in0=at[:], in1=bt[:], op=ALU.mult), [t_a, t_b], [t_y])
                fw.dma("sp", ycT[128 * c:128 * (c + 1), t0:t0 + TB], yt[:], [t_y], [])

    yaT = nc.dram_tensor("yaT", [BW, S], BF16, kind=okind).ap()
    ident = nc.alloc_sbuf_tensor("ident", [128, 128], F32)
    bdiag = nc.alloc_sbuf_tensor("bdiag", [128, 128], F32)
    LOs = nc.alloc_sbuf_tensor("LOs", [128, 128], F32)
    LOi = nc.alloc_sbuf_tensor("LOi", [128, 128], F32)
    UPi = nc.alloc_sbuf_tensor("UPi", [128, 128], F32)
    sel0 = nc.alloc_sbuf_tensor("sel0", [128, 128], F32)
    sel1 = nc.alloc_sbuf_tensor("sel1", [128, 128], F32)
    fw.op("pool", lambda e: e.affine_select(out=ident[:], in_=ones_f[:], pattern=[[-1, 128]], compare_op=ALU.is_equal,
                                            fill=0.0, base=0, channel_multiplier=1), [t_const], [t_const])
    fw.op("pool", lambda e: e.memset(bdiag[:], 0.0), [], [t_const])
    fw.op("pool", lambda e: e.memset(bdiag[0:64, 0:64], 1.0), [t_const], [t_const])
    fw.op("pool", lambda e: e.memset(bdiag[64:128, 64:128], 1.0), [t_const], [t_const])
    fw.op("pool", lambda e: e.memset(sel0[:], 0.0), [], [t_const])
    fw.op("pool", lambda e: e.memset(sel0[0:64, :], 1.0), [t_const], [t_const])
    fw.op("pool", lambda e: e.memset(sel1[:], 0.0), [], [t_const])
    fw.op("pool", lambda e: e.memset(sel1[64:128, :], 1.0), [t_const], [t_const])
    fw.op("pool", lambda e: e.affine_select(out=LOs[:], in_=bdiag[:], pattern=[[-1, 128]], compare_op=ALU.is_gt,
                                            fill=0.0, base=0, channel_multiplier=1), [t_const], [t_const])
    fw.op("pool", lambda e: e.affine_select(out=LOi[:], in_=bdiag[:], pattern=[[-1, 128]], compare_op=ALU.is_ge,
                                            fill=0.0, base=0, channel_multiplier=1), [t_const], [t_const])
    fw.op("pool", lambda e: e.affine_select(out=UPi[:], in_=bdiag[:], pattern=[[1, 128]], compare_op=ALU.is_ge,
                                            fill=0.0, base=0, channel_multiplier=-1), [t_const], [t_const])

    def phase_B(l):
        HW = HG * 128
        P0 = C.p_cq
        ab = sb("B_ab", [128, NT, 2 * HG], F32)
        t_ab = Tok()
        abv_ = ab_tm.rearrange("(t p) n -> p t n", p=128)
        for g_ in range(0, NT, 8):
            fw.dma("sp", ab[:, g_:min(NT, g_ + 8), :], abv_[:, g_:min(NT, g_ + 8), :], [], [t_ab])
        beta = sb("B_beta", [128, NT, HG], F32)
        gg = sb("B_g", [128, NT, HG], F32)
        z1 = sb("B_z1", [128, NT, HG], F32)
        z2 = sb("B_z2", [128, NT, HG], F32)
        eG = sb("B_eG", [128, NT, HG], F32)
        kdf = sb("B_kdf", [128, NT, HG], F32)
        gl0 = sb("B_gl0", [128, NT, HG], F32)
        gl1 = sb("B_gl1", [128, NT, HG], F32)
        nexp = sb("B_nexp", [128, HG], F32)
        t_g = Tok()
        act(fw, beta[:], ab[:, :, HG:2 * HG], AF.Sigmoid, [t_ab], [t_g])
        for hh in range(HG):
            fw.op("dve", lambda e: e.tensor_scalar(out=gg[:, :, hh], in0=ab[:, :, hh],
                                                   scalar1=prm_sb[:, l, C.p_dt + hh:C.p_dt + hh + 1], scalar2=None,
                                                   op0=ALU.add), [t_ab, t_prm], [t_g])
        fw.op("dve", lambda e: e.tensor_scalar(out=z1[:], in0=gg[:], scalar1=0.0, scalar2=None, op0=ALU.max),
              [t_g], [t_g])
        fw.op("dve", lambda e: e.scalar_tensor_tensor(out=z2[:], in0=z1[:], scalar=-2.0, in1=gg[:], op0=ALU.mult,
                                                      op1=ALU.add), [t_g], [t_g])
        act(fw, z2[:], z2[:], AF.Exp, [t_g], [t_g])
        act(fw, z2[:], z2[:], AF.Ln, [t_g], [t_g], bias=1.0)
        fw.op("dve", lambda e: e.tensor_tensor(out=z2[:], in0=z2[:], in1=z1[:], op=ALU.add), [t_g], [t_g])
        act(fw, nexp[:], prm_sb[:, l, C.p_al:C.p_al + HG], AF.Exp, [t_prm], [t_g])
        for hh in range(HG):
            fw.op("dve", lambda e: e.tensor_scalar(out=gg[:, :, hh], in0=z2[:, :, hh], scalar1=nexp[:, hh:hh + 1],
                                                   scalar2=-1.0, op0=ALU.mult, op1=ALU.mult), [t_g], [t_g])
        NH = NT * HG
        ggf = gg[:].rearrange("p t h -> p (t h)")
        pg, t_pg = PS.next()
        fw.op("pe", lambda e: e.matmul(pg[:, 0:NH], UPi[:], ggf, start=True, stop=True), [t_g, t_const], [t_pg])
        act(fw, eG[:].rearrange("p t h -> p (t h)"), pg[:, 0:NH], AF.Exp, [t_pg], [t_g])
        pg2, t_pg2 = PS.next()
        fw.op("pe", lambda e: e.matmul(pg2[:, 0:NH], bdiag[:], ggf, start=True, stop=True), [t_g, t_const], [t_pg2])
        fw.op("dve", lambda e: e.tensor_copy(out=z1[:].rearrange("p t h -> p (t h)"), in_=pg[:, 0:NH]), [t_pg, t_g], [t_g])
        fw.op("dve", lambda e: e.tensor_tensor(out=kdf[:].rearrange("p t h -> p (t h)"), in0=pg2[:, 0:NH],
                                               in1=z1[:].rearrange("p t h -> p (t h)"), op=ALU.subtract),
              [t_pg2, t_g], [t_g])
        act(fw, kdf[:], kdf[:], AF.Exp, [t_g], [t_g])
        pg3, t_pg3 = PS.next()
        fw.op("pe", lambda e: e.matmul(pg3[:, 0:NH], sel0[:], ggf, start=True, stop=True), [t_g, t_const], [t_pg3])
        act(fw, gl0[:].rearrange("p t h -> p (t h)"), pg3[:, 0:NH], AF.Exp, [t_pg3], [t_g])
        pg4, t_pg4 = PS.next()
        fw.op("pe", lambda e: e.matmul(pg4[:, 0:NH], sel1[:], ggf, start=True, stop=True), [t_g, t_const], [t_pg4])
        act(fw, gl1[:].rearrange("p t h -> p (t h)"), pg4[:, 0:NH], AF.Exp, [t_pg4], [t_g])

        if B_STOP == 1:
            return
        Sst = sb("B_S", [128, HG, 128], F32)
        S16 = sb("B_S16", [128, HG, 128], BF16)
        t_S, t_S16 = Tok(), Tok()
        fw.op("dve", lambda e: e.memset(Sst[:], 0.0), [], [t_S])
        fw.op("dve", lambda e: e.memset(S16[:], 0.0), [], [t_S16])

        xr = RingS("B_x", 3, [128, TB + 3], F32)
        acr = RingS("B_acc", 2, [128, TB], F32)
        slr = RingS("B_sl", 2, [128, 3 * BC, TB], F32)
        gsr = RingS("B_gs", 2, [128, BC, TB], F32)
        yar = RingS("B_ya", 2, [128, BC, TB], BF16)
        wTr = RingS("B_wT", 2, [128, HG, 128], BF16)
        qdr = RingS("B_qd", 2, [128, HG, 128], BF16)
        aqr = RingS("B_aq", 2, [128, HG, 128], BF16)
        kdr = RingS("B_kd", 2, [128, HG, 128], BF16)
        uur = RingS("B_u", 2, [128, HG, 128], F32)
        def arr(name, shape, dt, n=1):
            return [(sb(name, shape, dt), [Tok() for _ in range(HG)]) for _ in range(n)]
        src4 = arr("B_src4", [128, HG, 4, 128], F32)[0]
        fm4 = arr("B_fm4", [128, HG, 4, 128], BF16)[0]
        vbb = arr("B_vb", [128, HG, 128], BF16)[0]
        kbg = arr("B_kbg", [128, HG, 128], BF16)[0]
        gsm = arr("B_gsm", [128, HG, 128], F32)[0]
        dm = arr("B_dm", [128, HG, 3, 128], F32)[0]
        aqf = arr("B_aqf", [128, HG, 128], F32)[0]
        ABs = arr("B_AB", [128, HG, 2, 128], F32, 2)
        TTs = arr("B_TT", [128, HG, 128], F32, 2)
        X16 = arr("B_X16", [128, HG, 4, 128], BF16, 2)
        T16 = arr("B_T16", [128, HG, 2, 128], BF16, 2)
        ssq = arr("B_ssq", [128, HG, 2], F32)[0]
        junk = sb("B_junk", [128, 128], BF16)
        t_junk = Tok()
        onr = RingS("B_on", 2, [128, HG, 128], F32)
        oss = RingS("B_oss", 2, [128, HG], F32)

        def load_block(tb):
            t0 = tb * TB
            sl, t_sl = slr.next()
            for ch in range(3 * BC):
                xt, t_x = xr.next()
                r0 = r_gq + 128 * ch
                if tb == 0:
                    fw.op("dve", lambda e: e.memset(xt[:, 0:3], 0.0), [], [t_x])
                    fw.dma("sp", xt[:, 3:TB + 3], projT[r0:r0 + 128, 0:TB], [], [t_x])
                else:
                    fw.dma("sp", xt[:], projT[r0:r0 + 128, t0 - 3:t0 + TB], [], [t_x])
                acc, t_a = acr.next()
                wcol = lambda tap: prm_sb[:, l, P0 + tap * 3 * BC + ch:P0 + tap * 3 * BC + ch + 1]
                fw.op("dve", lambda e: e.tensor_scalar(out=acc[:], in0=xt[:, 0:TB], scalar1=wcol(0), scalar2=None,
                                                       op0=ALU.mult), [t_x, t_prm], [t_a])
                for tap in (1, 2, 3):
                    fw.op("dve", lambda e: e.scalar_tensor_tensor(out=acc[:], in0=xt[:, tap:tap + TB], scalar=wcol(tap),
                                                                  in1=acc[:], op0=ALU.mult, op1=ALU.add),
                          [t_x, t_prm, t_a], [t_a])
                act(fw, sl[:, ch, :], acc[:], AF.Silu, [t_a], [t_sl])
            gs, t_gs = gsr.next()
            fw.dma("sp", gs[:], projT[r_gate:r_gate + BW, t0:t0 + TB].rearrange("(c p) s -> p c s", p=128), [], [t_gs])
            act(fw, gs[:], gs[:], AF.Silu, [t_gs], [t_gs])
            fw.op("pool", lambda e: e.tensor_scalar(out=gs[:], in0=gs[:], scalar1=prm_sb[:, l, C.p_gn:C.p_gn + 1],
                                                    scalar2=1.0, op0=ALU.mult, op1=ALU.mult), [t_gs, t_prm], [t_gs])
            return sl, t_sl, gs, t_gs

        def prep(tt, sl, t_sl, tc):
            wT, t_wT = wTr.next()
            qd, t_qd = qdr.next()
            aq, t_aq = aqr.next()
            kd, t_kd = kdr.next()
            uu, t_uu = uur.next()
            bcol = lambda hh: beta[:, tt, hh:hh + 1]
            for hh in range(HG):
                pt, t_pt = PS.next()
                for i, ch in enumerate((hh, BC + hh, 2 * BC + hh)):
                    fw.op("pe", lambda e: e.transpose(pt[:, i * 128:(i + 1) * 128], sl[:, ch, tc:tc + 128], ident[:]),
                          [t_sl, t_const], [t_pt])
                sq, t_sq = ssq[0][:, hh, :], ssq[1][hh]
                act(fw, junk[:], pt[:, 0:128], AF.Square, [t_pt], [t_junk, t_sq], accum_out=sq[:, 0:1])
                act(fw, junk[:], pt[:, 128:256], AF.Square, [t_pt], [t_junk, t_sq], accum_out=sq[:, 1:2])
                act(fw, sq, sq, AF.Sqrt, [t_sq], [t_sq], bias=EPS)
                fw.op("dve", lambda e: e.reciprocal(out=sq, in_=sq), [t_sq], [t_sq])
                s4, t_s4 = src4[0][:, hh], src4[1][hh]
                fw.op("dve", lambda e: e.tensor_scalar(out=s4[:, 0, :], in0=pt[:, 128:256], scalar1=sq[:, 1:2],
                                                       scalar2=None, op0=ALU.mult), [t_pt, t_sq], [t_s4])
                fw.op("dve", lambda e: e.tensor_scalar(out=s4[:, 2, :], in0=pt[:, 0:128], scalar1=sq[:, 0:1],
                                                       scalar2=float(128 ** -0.5), op0=ALU.mult, op1=ALU.mult),
                      [t_pt, t_sq], [t_s4])
                fw.op("dve", lambda e: e.tensor_scalar(out=vbb[0][:, hh, :], in0=pt[:, 256:384], scalar1=bcol(hh),
                                                       scalar2=None, op0=ALU.mult), [t_pt, t_g], [vbb[1][hh]])
                fw.op("pool", lambda e: e.tensor_scalar(out=s4[:, 1, :], in0=s4[:, 0, :], scalar1=bcol(hh),
                                                        scalar2=1.0, op0=ALU.mult, op1=ALU.mult), [t_s4, t_g], [t_s4])
                fw.op("pool", lambda e: e.tensor_scalar(out=s4[:, 3, :], in0=s4[:, 2, :], scalar1=eG[:, tt, hh:hh + 1],
                                                        scalar2=1.0, op0=ALU.mult, op1=ALU.mult), [t_s4, t_g], [t_s4])
                fw.op("pool", lambda e: e.tensor_scalar(out=kbg[0][:, hh, :], in0=s4[:, 1, :],
                                                        scalar1=eG[:, tt, hh:hh + 1], scalar2=1.0, op0=ALU.mult,
                                                        op1=ALU.mult), [t_s4, t_g], [kbg[1][hh]])
                fw.op("pool", lambda e: e.tensor_scalar(out=kd[:, hh, :], in0=s4[:, 0, :],
                                                        scalar1=kdf[:, tt, hh:hh + 1], scalar2=1.0, op0=ALU.mult,
                                                        op1=ALU.mult), [t_s4, t_g], [t_kd])
                fw.op("pool", lambda e: e.tensor_scalar(out=gsm[0][:, hh, :], in0=LOs[:], scalar1=gg[:, tt, hh:hh + 1],
                                                        scalar2=1.0, op0=ALU.mult, op1=ALU.mult),
                      [t_const, t_g], [gsm[1][hh]])
            if B_STOP == 31:
                return None
            for hh in range(HG):
                s4, t_s4 = src4[0][:, hh], src4[1][hh]
                p4, t_p4 = PS.next()
                for i in range(4):
                    fw.op("pe", lambda e: e.transpose(p4[:, i * 128:(i + 1) * 128], s4[:, i, :], ident[:]),
                          [t_s4, t_const], [t_p4])
                f4, t_f4 = fm4[0][:, hh], fm4[1][hh]
                act(fw, f4.rearrange("p a b -> p (a b)"), p4[:, 0:512], AF.Copy, [t_p4], [t_f4])
            if B_STOP == 32:
                return None
            for hh in range(HG):
                f4, t_f4 = fm4[0][:, hh], fm4[1][hh]
                p5, t_p5 = PS.next()
                fw.op("pe", lambda e: e.matmul(p5[:, 0:128], f4[:, 1, :], f4[:, 0, :], start=True, stop=True), [t_f4], [t_p5])
                fw.op("pe", lambda e: e.matmul(p5[:, 128:256], f4[:, 2, :], f4[:, 0, :], start=True, stop=True), [t_f4], [t_p5])
                fw.op("pe", lambda e: e.matmul(p5[:, 256:384], UPi[:], gsm[0][:, hh, :], start=True, stop=True),
                      [gsm[1][hh], t_const], [t_p5])
                d3, t_d3 = dm[0][:, hh], dm[1][hh]
                act(fw, d3[:, 0, :], p5[:, 256:384], AF.Exp, [t_p5], [t_d3])
                fw.op("pool", lambda e: e.tensor_tensor(out=d3[:, 1, :], in0=d3[:, 0, :], in1=LOs[:], op=ALU.mult),
                      [t_d3, t_const], [t_d3])
                fw.op("pool", lambda e: e.tensor_tensor(out=d3[:, 2, :], in0=d3[:, 0, :], in1=LOi[:], op=ALU.mult),
                      [t_d3, t_const], [t_d3])
                AB, t_AB = ABs[0][0][:, hh], ABs[0][1][hh]
                fw.op("dve", lambda e: e.tensor_tensor(out=AB[:, 0, :], in0=p5[:, 0:128], in1=d3[:, 1, :], op=ALU.mult),
                      [t_p5, t_d3], [t_AB])
                fw.op("dve", lambda e: e.tensor_tensor(out=aqf[0][:, hh, :], in0=p5[:, 128:256], in1=d3[:, 2, :],
                                                       op=ALU.mult), [t_p5, t_d3], [aqf[1][hh]])
            if B_STOP == 33:
                return None
            for hh in range(HG):
                AB, t_AB = ABs[0][0][:, hh], ABs[0][1][hh]
                p6, t_p6 = PS.next()
                fw.op("pe", lambda e: e.transpose(p6[:, 0:128], AB[:, 0, :], ident[:]), [t_AB, t_const], [t_p6])
                fw.op("pe", lambda e: e.transpose(p6[:, 128:256], aqf[0][:, hh, :], ident[:]), [aqf[1][hh], t_const], [t_p6])
                X0, t_X0 = X16[0][0][:, hh], X16[0][1][hh]
                act(fw, X0[:, 0, :], AB[:, 0, :], AF.Copy, [t_AB], [t_X0])
                fw.op("pool", lambda e: e.tensor_tensor(out=X0[:, 2, :], in0=AB[:, 0, :], in1=X0[:, 0, :], op=ALU.subtract),
                      [t_AB, t_X0], [t_X0])
                act(fw, X0[:, 1, :], p6[:, 0:128], AF.Copy, [t_p6], [t_X0])
                fw.op("dve", lambda e: e.tensor_tensor(out=X0[:, 3, :], in0=p6[:, 0:128], in1=X0[:, 1, :], op=ALU.subtract),
                      [t_p6, t_X0], [t_X0])
                TT0, t_TT0 = TTs[0][0][:, hh, :], TTs[0][1][hh]
                fw.op("dve", lambda e: e.tensor_tensor(out=TT0, in0=ident[:], in1=p6[:, 0:128], op=ALU.subtract),
                      [t_p6, t_const], [t_TT0])
                Th0, t_Th0 = T16[0][0][:, hh], T16[0][1][hh]
                act(fw, Th0[:, 0, :], TT0, AF.Copy, [t_TT0], [t_Th0])
                fw.op("pool", lambda e: e.tensor_tensor(out=Th0[:, 1, :], in0=TT0, in1=Th0[:, 0, :], op=ALU.subtract),
                      [t_TT0, t_Th0], [t_Th0])
                act(fw, aq[:, hh, :], p6[:, 128:256], AF.Copy, [t_p6], [t_aq])
            if B_STOP == 34:
                return None
            fin = [0] * HG
            for hh in range(HG):
                cur = 0
                for lev in range(1, 6):
                    nxt = 1 - cur
                    X, t_X = X16[cur][0][:, hh], X16[cur][1][hh]
                    X2, t_X2 = X16[nxt][0][:, hh], X16[nxt][1][hh]
                    p7, t_p7 = PD7
                    for (wi, mi, st_, sp_) in ((1, 0, True, False), (1, 2, False, False), (3, 0, False, True)):
                        fw.op("pe", lambda e: e.matmul(p7[:, 0:128], X[:, wi, :], X[:, mi, :], start=st_, stop=sp_), [t_X], [t_p7])
                    ncol = 128
                    if lev < 5:
                        for (wi, mi, st_, sp_) in ((0, 1, True, False), (0, 3, False, False), (2, 1, False, True)):
                            fw.op("pe", lambda e: e.matmul(p7[:, 128:256], X[:, wi, :], X[:, mi, :], start=st_, stop=sp_), [t_X], [t_p7])
                        ncol = 256
                    X2f = X2.rearrange("p a b -> p (a b)")
                    act(fw, X2f[:, 0:ncol], p7[:, 0:ncol], AF.Copy, [t_p7], [t_X2])
                    fw.op("dve", lambda e: e.tensor_tensor(out=X2f[:, 256:256 + ncol], in0=p7[:, 0:ncol], in1=X2f[:, 0:ncol],
                                                           op=ALU.subtract), [t_p7, t_X2], [t_X2])
                    T0, t_T0 = TTs[cur][0][:, hh, :], TTs[cur][1][hh]
                    T1, t_T1 = TTs[nxt][0][:, hh, :], TTs[nxt][1][hh]
                    Th, t_Th = T16[cur][0][:, hh], T16[cur][1][hh]
                    Th2, t_Th2 = T16[nxt][0][:, hh], T16[nxt][1][hh]
                    p8, t_p8 = PD8
                    for (wi, mi, st_, sp_) in ((0, 0, True, False), (0, 1, False, False), (2, 0, False, True)):
                        fw.op("pe", lambda e: e.matmul(p8[:, 0:128], X2[:, wi, :], Th[:, mi, :], start=st_, stop=sp_),
                              [t_X2, t_Th], [t_p8])
                    fw.op("dve", lambda e: e.tensor_tensor(out=T1, in0=p8[:, 0:128], in1=T0, op=ALU.add), [t_p8, t_T0], [t_T1])
                    act(fw, Th2[:, 0, :], T1, AF.Copy, [t_T1], [t_Th2])
                    if lev < 5:
                        fw.op("pool", lambda e: e.tensor_tensor(out=Th2[:, 1, :], in0=T1, in1=Th2[:, 0, :], op=ALU.subtract),
                              [t_T1, t_Th2], [t_Th2])
                    cur = nxt
                fin[hh] = cur
            cur = fin[0]
            if B_STOP == 35:
                return None
            for hh in range(HG):
                tb_, t_tb = T16[cur][0][:, hh, 0, :], T16[cur][1][hh]
                p9, t_p9 = PS.next()
                fw.op("pe", lambda e: e.matmul(p9[:, 0:128], tb_, vbb[0][:, hh, :], start=True, stop=True),
                      [t_tb, vbb[1][hh]], [t_p9])
                fw.op("pe", lambda e: e.matmul(p9[:, 128:256], kbg[0][:, hh, :], tb_, start=True, stop=True),
                      [t_tb, kbg[1][hh]], [t_p9])
                fw.op("dve", lambda e: e.tensor_copy(out=uu[:, hh, :], in_=p9[:, 0:128]), [t_p9], [t_uu])
                act(fw, wT[:, hh, :], p9[:, 128:256], AF.Copy, [t_p9], [t_wT])
                fw.op("pool", lambda e: e.tensor_copy(out=qd[:, hh, :], in_=fm4[0][:, hh, 3, :]), [fm4[1][hh]], [t_qd])
            return (wT, t_wT, qd, t_qd, aq, t_aq, kd, t_kd, uu, t_uu)

        vnr = RingS("B_vn", 2, [128, HG, 128], BF16)

        def scan(tt, ops, gs, t_gs, ya, t_ya, tc):
            (wT, t_wT, qd, t_qd, aq, t_aq, kd, t_kd, uu, t_uu) = ops
            po, t_po = PSO.next()
            vn, t_vn = vnr.next()
            for x in (0, 1):
                r0, r1 = 64 * x, 64 * x + 64
                glx = gl0 if x == 0 else gl1
                p1, t_p1 = PS.next()
                for hh in range(HG):
                    fw.op("pe", lambda e: e.matmul(p1[r0:r1, hh * 128:(hh + 1) * 128], wT[:, hh, r0:r1], S16[:, hh, :],
                                                   start=True, stop=True), [t_wT, t_S16], [t_p1])
                fw.op("dve", lambda e: e.tensor_tensor(out=vn[r0:r1].rearrange("p a b -> p (a b)"),
                                                       in0=uu[r0:r1].rearrange("p a b -> p (a b)"),
                                                       in1=p1[r0:r1, 0:HW], op=ALU.subtract), [t_uu, t_p1], [t_vn])
                for hh in range(HG):
                    fw.op("pe", lambda e: e.matmul(po[r0:r1, hh * 128:(hh + 1) * 128], qd[:, hh, r0:r1], S16[:, hh, :],
                                                   start=True, stop=False), [t_qd, t_S16], [t_po])
                    fw.op("pe", lambda e: e.matmul(po[r0:r1, hh * 128:(hh + 1) * 128], aq[r0:r1, hh, r0:r1],
                                                   vn[r0:r1, hh, :], start=False, stop=True), [t_aq, t_vn], [t_po])
                p3, t_p3 = PS.next()
                for hh in range(HG):
                    fw.op("pe", lambda e: e.matmul(p3[:, hh * 128:(hh + 1) * 128], kd[r0:r1, hh, :], vn[r0:r1, hh, :],
                                                   start=True, stop=True), [t_kd, t_vn], [t_p3])
                for hh in range(HG):
                    fw.op("dve", lambda e: e.scalar_tensor_tensor(out=Sst[:, hh, :], in0=Sst[:, hh, :],
                                                                  scalar=glx[:, tt, hh:hh + 1],
                                                                  in1=p3[:, hh * 128:(hh + 1) * 128],
                                                                  op0=ALU.mult, op1=ALU.add), [t_S, t_p3, t_g], [t_S])
                act(fw, S16[:], Sst[:], AF.Copy, [t_S], [t_S16])
            os_, t_os = oss.next()
            on, t_on = onr.next()
            for hh in range(HG):
                act(fw, junk[:], po[:, hh * 128:(hh + 1) * 128], AF.Square, [t_po], [t_junk, t_os],
                    accum_out=os_[:, hh:hh + 1])
            act(fw, os_[:], os_[:], AF.Sqrt, [t_os], [t_os], scale=1.0 / 128, bias=EPS)
            fw.op("dve", lambda e: e.reciprocal(out=os_[:], in_=os_[:]), [t_os], [t_os])
            for hh in range(HG):
                fw.op("dve", lambda e: e.tensor_scalar(out=on[:, hh, :], in0=po[:, hh * 128:(hh + 1) * 128],
                                                       scalar1=os_[:, hh:hh + 1], scalar2=None, op0=ALU.mult),
                      [t_po, t_os], [t_on])
            pT, t_pT = PS.next()
            for hh in range(HG):
                fw.op("pe", lambda e: e.transpose(pT[:, hh * 128:(hh + 1) * 128], on[:, hh, :], ident[:]),
                      [t_on, t_const], [t_pT])
            fw.op("dve", lambda e: e.tensor_tensor(out=ya[:, :, tc:tc + 128],
                                                   in0=pT[:, 0:HW].rearrange("p (a b) -> p a b", a=HG),
                                                   in1=gs[:, :, tc:tc + 128], op=ALU.mult), [t_pT, t_gs], [t_ya])

        TPB = TB // 128
        blk = load_block(0)
        if B_STOP == 2:
            return
        pending = prep(0, blk[0], blk[1], 0)
        if B_STOP >= 3 and B_STOP != 4:
            return
        ya, t_ya = yar.next()
        for tt in range(NT):
            tb = tt // TPB
            cur_blk = blk
            cur_ops = pending
            if tt + 1 < NT:
                if (tt + 1) % TPB == 0:
                    blk = load_block(tb + 1)
                pending = prep(tt + 1, blk[0], blk[1], ((tt + 1) % TPB) * 128)
            scan(tt, cur_ops, cur_blk[2], cur_blk[3], ya, t_ya, (tt % TPB) * 128)
            if B_STOP == 4:
                return
            if (tt + 1) % TPB == 0:
                fw.dma("sp", yaT[:, tb * TB:(tb + 1) * TB].rearrange("(c p) s -> p c s", p=128), ya[:], [t_ya], [])
                ya, t_ya = yar.next()

    def rms_rstd(src, t_src, sq_ring, rs_ring):
        ps, t_ps = PS.next()
        for kc in range(KC):
            sq, t_sq = sq_ring.next()
            act(fw, sq[:], src[:, kc, :], AF.Square, [t_src], [t_sq])
            fw.op("pe", lambda e: e.matmul(ps[:, 0:TB], ones_f[:], sq[:], start=(kc == 0), stop=(kc == KC - 1)),
                  [t_sq, t_const], [t_ps])
        r, t_r = rs_ring.next()
        act(fw, r[:], ps[:, 0:TB], AF.Sqrt, [t_ps], [t_r], scale=1.0 / D, bias=EPS)
        fw.op("dve", lambda e: e.reciprocal(out=r[:], in_=r[:]), [t_r], [t_r])
        return r, t_r

    xmid = nc.dram_tensor("xmid", [D, S], F32, kind=okind).ap()
    xa = nc.dram_tensor("xa", [D, S], F32, kind=okind).ap()

    def phase_E(l, xsrc):
        wbb = sb("E_wbb", [128, 3, BC, D], BF16)
        wob = sb("E_wob", [128, KC, D], BF16)
        t_wb, t_wo = Tok(), Tok()
        wst = RingS("E_wst", 3, [128, D], F32)
        for i in range(3):
            for kc in range(BC):
                ws, t_ws = wst.next()
                fw.dma("sp", ws[:], w_br[l, i, kc * 128:(kc + 1) * 128, :], [], [t_ws])
                fw.op("pool", lambda e: e.tensor_copy(out=wbb[:, i, kc, :], in_=ws[:]), [t_ws], [t_wb])
        for kc in range(KC):
            ws, t_ws = wst.next()
            fw.dma("sp", ws[:], w_out[l, kc * 128:(kc + 1) * 128, :], [], [t_ws])
            fw.op("pool", lambda e: e.tensor_copy(out=wob[:, kc, :], in_=ws[:]), [t_ws], [t_wo])
        yr = RingS("E_y", 2, [128, 3, BC, TB], BF16)
        gr = RingS("E_g", 4, [128, TB], F32)
        tr = RingS("E_t", 3, [128, TB], F32)
        mgr = RingS("E_mg", 2, [128, KC, TB], BF16)
        m2r = RingS("E_m2", 1, [128, KC, TB], F32)
        xr = RingS("E_x", 2, [128, KC, TB], F32)
        sqr = RingS("E_sq", 2, [128, TB], F32)
        rsr = RingS("E_rs", 2, [128, TB], F32)
        ysrc = (yaT, ybT, ycT)
        xv = xsrc.rearrange("(c p) s -> p c s", p=128)
        xo = xmid.rearrange("(c p) s -> p c s", p=128)
        nev = 0
        for tb in range(NTB):
            t0 = tb * TB
            y, t_y = yr.next()
            for i in range(3):
                fw.dma("sp", y[:, i], ysrc[i][:, t0:t0 + TB].rearrange("(c p) s -> p c s", p=128), [], [t_y])
            xt, t_x = xr.next()
            fw.dma("sp", xt[:], xv[:, :, t0:t0 + TB], [], [t_x])
            mg, t_mg = mgr.next()
            for oc in range(KC):
                acc, t_acc = tr.next()
                for i in range(3):
                    g, t_gt = gr.next()
                    rg = r_gates + i * D + oc * 128
                    fw.dma("sp", g[:], projT[rg:rg + 128, t0:t0 + TB], [], [t_gt])
                    act(fw, g[:], g[:], AF.Sigmoid, [t_gt], [t_gt])
                    ps, t_ps = PS.next()
                    for kc in range(BC):
                        fw.op("pe", lambda e: e.matmul(ps[:, 0:TB], wbb[:, i, kc, oc * 128:(oc + 1) * 128], y[:, i, kc, :],
                                                       start=(kc == 0), stop=(kc == BC - 1)), [t_wb, t_y], [t_ps])
                    if i == 0:
                        fw.op("dve", lambda e: e.tensor_tensor(out=acc[:], in0=ps[:, 0:TB], in1=g[:], op=ALU.mult),
                              [t_ps, t_gt], [t_acc])
                    else:
                        fw.op("dve", lambda e: e.tensor_tensor(out=g[:], in0=ps[:, 0:TB], in1=g[:], op=ALU.mult),
                              [t_ps, t_gt], [t_gt])
                        if i == 1:
                            fw.op("pool", lambda e: e.tensor_tensor(out=acc[:], in0=acc[:], in1=g[:], op=ALU.add),
                                  [t_acc, t_gt], [t_acc])
                        else:
                            fw.op("pool", lambda e: e.tensor_tensor(out=mg[:, oc, :], in0=acc[:], in1=g[:], op=ALU.add),
                                  [t_acc, t_gt], [t_mg])
            m2, t_m2 = m2r.next()
            for oc in range(KC):
                ps, t_ps = PS.next()
                for kc in range(KC):
                    fw.op("pe", lambda e: e.matmul(ps[:, 0:TB], wob[:, kc, oc * 128:(oc + 1) * 128], mg[:, kc, :],
                                                   start=(kc == 0), stop=(kc == KC - 1)), [t_wo, t_mg], [t_ps])
                if nev % 2 == 0:
                    act(fw, m2[:, oc, :], ps[:, 0:TB], AF.Copy, [t_ps], [t_m2])
                else:
                    fw.op("dve", lambda e: e.tensor_copy(out=m2[:, oc, :], in_=ps[:, 0:TB]), [t_ps], [t_m2])
                nev += 1
            r, t_r = rms_rstd(m2, t_m2, sqr, rsr)
            for kc in range(KC):
                fw.op("dve", lambda e: e.scalar_tensor_tensor(out=m2[:, kc, :], in0=m2[:, kc, :],
                                                              scalar=prm_sb[:, l, C.p_n2 + kc:C.p_n2 + kc + 1], in1=r[:],
                                                              op0=ALU.mult, op1=ALU.mult), [t_m2, t_r, t_prm], [t_m2])
            fw.op("pool", lambda e: e.tensor_tensor(out=xt[:], in0=xt[:], in1=m2[:], op=ALU.add), [t_x, t_m2], [t_x])
            fw.dma("sp", xo[:, :, t0:t0 + TB], xt[:], [t_x], [])

    def phase_F(l, xdst):
        FC = C.FC
        xr = RingS("F_x", 2, [128, KC, TB], F32)
        h2r = RingS("F_h2", 1, [128, KC, TB], BF16)
        ur = RingS("F_u", 1, [128, FC, TB], BF16)
        w1s = RingS("F_w1s", 2, [128, KC, 128], F32)
        w1b = RingS("F_w1b", 2, [128, KC, 128], BF16)
        w2s = RingS("F_w2s", 2, [128, FC, 128], F32)
        w2b = RingS("F_w2b", 2, [128, FC, 128], BF16)
        rr = RingS("F_r", 2, [128, TB], F32)
        m2r = RingS("F_m2", 1, [128, KC, TB], F32)
        sqr = RingS("F_sq", 2, [128, TB], F32)
        rsr = RingS("F_rs", 2, [128, TB], F32)
        xv = xmid.rearrange("(c p) s -> p c s", p=128)
        xo = xdst.rearrange("(c p) s -> p c s", p=128)
        w1v = w_ff1[l].rearrange("(c p) n -> p c n", p=128)
        w2v = w_ff2[l].rearrange("(c p) n -> p c n", p=128)
        nev = 0
        for tb in range(NTB):
            t0 = tb * TB
            xt, t_x = xr.next()
            fw.dma("sp", xt[:], xv[:, :, t0:t0 + TB], [], [t_x])
            r, t_r = rms_rstd(xt, t_x, sqr, rsr)
            h2, t_h2 = h2r.next()
            for kc in range(KC):
                fw.op("dve", lambda e: e.scalar_tensor_tensor(out=h2[:, kc, :], in0=xt[:, kc, :],
                                                              scalar=prm_sb[:, l, C.p_n3 + kc:C.p_n3 + kc + 1], in1=r[:],
                                                              op0=ALU.mult, op1=ALU.mult), [t_x, t_r, t_prm], [t_h2])
            u, t_u = ur.next()
            for fc in range(FC):
                ws, t_ws = w1s.next()
                fw.dma("sp", ws[:], w1v[:, :, fc * 128:(fc + 1) * 128], [], [t_ws])
                wb, t_wb = w1b.next()
                fw.op("pool", lambda e: e.tensor_copy(out=wb[:], in_=ws[:]), [t_ws], [t_wb])
                ps, t_ps = PS.next()
                for kc in range(KC):
                    fw.op("pe", lambda e: e.matmul(ps[:, 0:TB], wb[:, kc, :], h2[:, kc, :], start=(kc == 0),
                                                   stop=(kc == KC - 1)), [t_wb, t_h2], [t_ps])
                rl, t_rl = rr.next()
                act(fw, rl[:], ps[:, 0:TB], AF.Relu, [t_ps], [t_rl])
                fw.op("pool", lambda e: e.tensor_tensor(out=u[:, fc, :], in0=rl[:], in1=rl[:], op=ALU.mult), [t_rl], [t_u])
            m2, t_m2 = m2r.next()
            for oc in range(KC):
                ws, t_ws = w2s.next()
                fw.dma("sp", ws[:], w2v[:, :, oc * 128:(oc + 1) * 128], [], [t_ws])
                wb, t_wb = w2b.next()
                fw.op("pool", lambda e: e.tensor_copy(out=wb[:], in_=ws[:]), [t_ws], [t_wb])
                ps, t_ps = PS.next()
                for fc in range(FC):
                    fw.op("pe", lambda e: e.matmul(ps[:, 0:TB], wb[:, fc, :], u[:, fc, :], start=(fc == 0),
                                                   stop=(fc == FC - 1)), [t_wb, t_u], [t_ps])
                if nev % 2 == 0:
                    act(fw, m2[:, oc, :], ps[:, 0:TB], AF.Copy, [t_ps], [t_m2])
                else:
                    fw.op("dve", lambda e: e.tensor_copy(out=m2[:, oc, :], in_=ps[:, 0:TB]), [t_ps], [t_m2])
                nev += 1
            r2, t_r2 = rms_rstd(m2, t_m2, sqr, rsr)
            for kc in range(KC):
                fw.op("dve", lambda e: e.scalar_tensor_tensor(out=m2[:, kc, :], in0=m2[:, kc, :],
                                                              scalar=prm_sb[:, l, C.p_n4 + kc:C.p_n4 + kc + 1], in1=r2[:],
                                                              op0=ALU.mult, op1=ALU.mult), [t_m2, t_r2, t_prm], [t_m2])
            fw.op("pool", lambda e: e.tensor_tensor(out=xt[:], in0=xt[:], in1=m2[:], op=ALU.add), [t_x, t_m2], [t_x])
            fw.dma("sp", xo[:, :, t0:t0 + TB], xt[:], [t_x], [])


    def run_phase(fn, *a):
        with contextlib.ExitStack() as es:
            st["es"] = es
            fn(*a)
            fw.barrier()
        st["es"] = None

    for l in range(C.NL):
        xsrc = xin if l == 0 else xa
        xdst = yout if l == C.NL - 1 else xa
        if "A" in phases:
            run_phase(phase_A, l, xsrc)
        if "B" in phases:
            run_phase(phase_B, l)
        if "C" in phases:
            run_phase(phase_C, l)
        if "D" in phases:
            run_phase(phase_D, l)
        if "E" in phases:
            run_phase(phase_E, l, xsrc)
        if "F" in phases:
            run_phase(phase_F, l, xdst)
    fw.barrier(("sp",))
    print("instructions", fw.nins, "waits", fw.nwait, "rot", fw.nrot)
    return nc


def pack_params(C, inp):
    NL = C.NL
    out = np.zeros((NL, 128, C.NP), np.float32)
    for l in range(NL):
        def fm(v):
            return np.ascontiguousarray(v.reshape(-1, 128).T)
        out[l, :, C.p_n1:C.p_n1 + C.KC] = fm(inp["norm_mix_pre"][l])
        out[l, :, C.p_n2:C.p_n2 + C.KC] = fm(inp["norm_mix_post"][l])
        out[l, :, C.p_n3:C.p_n3 + C.KC] = fm(inp["norm_ffn_pre"][l])
        out[l, :, C.p_n4:C.p_n4 + C.KC] = fm(inp["norm_ffn_post"][l])
        cq = inp["conv_qkv_w"][l]
        out[l, :, C.p_cq:C.p_cq + 12 * C.BC] = cq.reshape(4, 3 * C.BC, 128).transpose(2, 0, 1).reshape(128, -1)
        cs = inp["conv_sc_w"][l]
        out[l, :, C.p_cs:C.p_cs + 3 * C.BC] = cs.reshape(3, C.BC, 128).transpose(2, 0, 1).reshape(128, -1)
        out[l, :, C.p_gn] = inp["gdn_norm_w"][l]
        out[l, :, C.p_al:C.p_al + C.HG] = np.broadcast_to(inp["gdn_a_log"][l][None, :], (128, C.HG))
        out[l, :, C.p_dt:C.p_dt + C.HG] = np.broadcast_to(inp["gdn_dt_bias"][l][None, :], (128, C.HG))
    return out


def make_in_maps(C, inp, nb):
    prm = pack_params(C, inp)
    maps = []
    for b in range(nb):
        maps.append({
            "xT": np.ascontiguousarray(np.asarray(inp["x"][b]).T),
            "w_in": np.asarray(inp["w_in"]), "w_branch": np.asarray(inp["w_branch"]),
            "w_out": np.asarray(inp["w_out"]), "w_ff1": np.asarray(inp["w_ff1"]),
            "w_ff2": np.asarray(inp["w_ff2"]), "prm": prm,
        })
    return maps


def kernel(**inputs):
    C = Cfg()
    inp = {k: np.asarray(v) for k, v in inputs.items()}
    nc = build(C, phases=("A", "B", "C", "D", "E", "F"))
    maps = make_in_maps(C, inp, 8)
    res = run_bass_kernel_spmd(nc, maps, core_ids=list(range(8)))
    out = np.stack([np.ascontiguousarray(r["yT"].T) for r in res.results], axis=0)
    return out.astype(np.float32)
```

```python
import numpy as np
import concourse.bass as bass
import concourse.mybir as mybir
from concourse.bass_utils import run_bass_kernel_spmd

F32 = mybir.dt.float32
BF16 = mybir.dt.bfloat16
AF = mybir.ActivationFunctionType
ALU = mybir.AluOpType
EPS = 1e-6
B_STOP = 0
DBL_H = 0
DBL_L = 5
DBL_BAR = 0
DBL_PARTS = 4


class Cfg:
    def __init__(self, S=4096, D=1024, HG=4, HS=8, DFF=4096, NL=4, TB=512):
        self.S, self.D, self.HG, self.HS, self.DFF, self.NL = S, D, HG, HS, DFF, NL
        self.BW = 128 * HG
        assert self.BW == 64 * HS
        self.KC = D // 128
        self.BC = self.BW // 128
        self.FC = DFF // 128
        self.TB = min(TB, S)
        self.NTB = S // self.TB
        self.NT = S // 128
        BW = self.BW
        self.o_gq, self.o_gate = 0, 3 * BW
        self.o_a, self.o_b = 4 * BW, 4 * BW + HG
        self.o_sq = 4 * BW + 2 * HG
        self.o_scx = self.o_sq + 3 * BW
        self.o_gates = self.o_scx + 3 * BW
        self.NIN = self.o_gates + 3 * D
        KC, BC = self.KC, self.BC
        c = 0
        self.p_n1 = c; c += KC
        self.p_n2 = c; c += KC
        self.p_n3 = c; c += KC
        self.p_n4 = c; c += KC
        self.p_cq = c; c += 4 * 3 * BC
        self.p_cs = c; c += 3 * BC
        self.p_gn = c; c += 1
        self.p_al = c; c += HG
        self.p_dt = c; c += HG
        self.NP = c


class Tok:
    __slots__ = ("w", "r")

    def __init__(self):
        self.w = None
        self.r = []


class FW:
    def __init__(self, nc, nds=40):
        self.nc = nc
        self.eng = {"pe": nc.tensor, "act": nc.scalar, "dve": nc.vector, "pool": nc.gpsimd, "sp": nc.sync}
        self.sem = {e: nc.alloc_semaphore(name="prog_" + e) for e in self.eng}
        self.cnt = {e: 0 for e in self.eng}
        self.seen = {e: {} for e in self.eng}
        self.dsem = [nc.alloc_semaphore(name="dsem%d" % i) for i in range(nds)]
        self.dval = [0] * nds
        self.drr = 0
        self.nwait = 0
        self.nins = 0
        self.nrot = 0
        self.old = []

    def _wait(self, e, ev, kind):
        if ev is None:
            return
        sem, val, src = ev
        if src == e and (e == "pe" or kind == "war"):
            return
        key = sem.num
        if self.seen[e].get(key, 0) >= val:
            return
        self.eng[e].wait_ge(sem, val)
        self.seen[e][key] = val
        self.nwait += 1

    def _deps(self, e, reads, writes):
        for t in reads:
            self._wait(e, t.w, "raw")
        for t in writes:
            self._wait(e, t.w, "waw")
            for ev in t.r:
                self._wait(e, ev, "war")

    def _mark(self, ev, reads, writes):
        for t in reads:
            t.r = [x for x in t.r if x[0].num != ev[0].num] + [ev]
        for t in writes:
            t.w = ev
            t.r = []

    def op(self, e, fn, reads=(), writes=()):
        self._deps(e, reads, writes)
        ins = fn(self.eng[e])
        if self.cnt[e] >= 30000:
            self.nrot += 1
            self.old.append((self.sem[e], self.cnt[e], e))
            self.sem[e] = self.nc.alloc_semaphore(name="prog_%s_%d" % (e, self.nrot))
            self.cnt[e] = 0
        self.cnt[e] += 1
        ins.then_inc(self.sem[e], 1)
        ev = (self.sem[e], self.cnt[e], e)
        self._mark(ev, reads, writes)
        self.nins += 1
        return ev

    def dma(self, q, out, in_, reads=(), writes=()):
        self._deps(q, reads, writes)
        i = self.drr
        self.drr = (self.drr + 1) % len(self.dsem)
        self._wait(q, (self.dsem[i], self.dval[i], "dma"), "raw")
        self.dval[i] += 16
        self.eng[q].dma_start(out=out, in_=in_).then_inc(self.dsem[i], 16)
        ev = (self.dsem[i], self.dval[i], "dma")
        self._mark(ev, reads, writes)
        self.nins += 1
        return ev

    def barrier(self, engines=("pe", "act", "dve", "pool", "sp")):
        for e in engines:
            for ev in self.old:
                if ev[2] != e:
                    self._wait(e, ev, "raw")
            for f in self.eng:
                if f != e and self.cnt[f] > 0:
                    self._wait(e, (self.sem[f], self.cnt[f], f), "raw")
            for i in range(len(self.dsem)):
                if self.dval[i] > 0:
                    self._wait(e, (self.dsem[i], self.dval[i], "dma"), "raw")


class Ring:
    def __init__(self, alloc, name, n, shape, dt):
        self.bufs = []
        for i in range(n):
            self.bufs.append((alloc("%s_%d" % (name, i), shape, dt), Tok()))
        self.i = 0

    def next(self):
        b = self.bufs[self.i]
        self.i = (self.i + 1) % len(self.bufs)
        return b


def act(fw, out, in_, func, reads, writes, **kw):
    return fw.op("act", lambda e: e.activation(out=out, in_=in_, func=func, **kw), reads, writes)


def build(cfg, phases=("A",), dump=False):
    C = cfg
    nc = bass.Bass("TRN2", target_bir_lowering=False)
    S, D, KC, BW, BC, TB, NTB, NT, HG, HS = C.S, C.D, C.KC, C.BW, C.BC, C.TB, C.NTB, C.NT, C.HG, C.HS
    xin = nc.dram_tensor("xT", [D, S], F32, kind="ExternalInput").ap()
    w_in = nc.dram_tensor("w_in", [C.NL, D, C.NIN], F32, kind="ExternalInput").ap()
    w_br = nc.dram_tensor("w_branch", [C.NL, 3, BW, D], F32, kind="ExternalInput").ap()
    w_out = nc.dram_tensor("w_out", [C.NL, D, D], F32, kind="ExternalInput").ap()
    w_ff1 = nc.dram_tensor("w_ff1", [C.NL, D, C.DFF], F32, kind="ExternalInput").ap()
    w_ff2 = nc.dram_tensor("w_ff2", [C.NL, C.DFF, D], F32, kind="ExternalInput").ap()
    prm = nc.dram_tensor("prm", [C.NL, 128, C.NP], F32, kind="ExternalInput").ap()
    yout = nc.dram_tensor("yT", [D, S], F32, kind="ExternalOutput").ap()
    okind = "ExternalOutput" if dump else "Internal"
    NFM = 4 * BW + 3 * BW + 3 * D
    projT = nc.dram_tensor("projT", [NFM, S], F32, kind=okind).ap()
    r_gq, r_gate, r_sc, r_gates = 0, 3 * BW, 4 * BW, 7 * BW
    sqkT = nc.dram_tensor("sqkT", [2 * BW, S], BF16, kind=okind).ap()
    sv_tm = nc.dram_tensor("sv_tm", [S, BW], BF16, kind=okind).ap()
    ab_tm = nc.dram_tensor("ab_tm", [S, 2 * HG], F32, kind=okind).ap()

    fw = FW(nc)
    import contextlib
    st = {"es": None, "n": 0}

    def sb(name, shape, dt):
        st["n"] += 1
        return st["es"].enter_context(nc.sbuf_tensor("%s_%d" % (name, st["n"]), shape, dt))

    def psa(name, shape, dt):
        return nc.alloc_psum_tensor(name, shape, dt)

    def RingS(name, n, shape, dt):
        return Ring(sb, name, n, shape, dt)

    ones_f = nc.alloc_sbuf_tensor("ones_f", [128, 128], F32)
    t_const = Tok()
    fw.op("dve", lambda e: e.memset(ones_f[:], 1.0), [], [t_const])
    prm_sb = nc.alloc_sbuf_tensor("prm_sb", [128, C.NL, C.NP], F32)
    t_prm = Tok()
    for l in range(C.NL):
        fw.dma("sp", prm_sb[:, l, :], prm[l], [], [t_prm])
    PS = Ring(psa, "ps", 4, [128, 512], F32)
    PD7 = (psa("pd7", [128, 512], F32), Tok())
    PD8 = (psa("pd8", [128, 512], F32), Tok())
    PSO = Ring(psa, "pso", 2, [128, 512], F32)

    def phase_A(l, xsrc):
        h = sb("A_h", [128, KC, S], BF16)
        t_h = [Tok() for _ in range(NTB)]
        xs = RingS("A_xs", 2, [128, KC, TB], F32)
        sqt = RingS("A_sq", 2, [128, TB], F32)
        rs = RingS("A_rs", 2, [128, TB], F32)
        xv = xsrc.rearrange("(c p) s -> p c s", p=128)
        for tb in range(NTB):
            t0 = tb * TB
            xt, t_x = xs.next()
            fw.dma("sp", xt[:], xv[:, :, t0:t0 + TB], [], [t_x])
            ps, t_ps = PS.next()
            for kc in range(KC):
                sq, t_sq = sqt.next()
                act(fw, sq[:], xt[:, kc, :], AF.Square, [t_x], [t_sq])
                fw.op("pe", lambda e: e.matmul(ps[:, 0:TB], ones_f[:], sq[:], start=(kc == 0), stop=(kc == KC - 1)),
                      [t_sq, t_const], [t_ps])
            r, t_r = rs.next()
            act(fw, r[:], ps[:, 0:TB], AF.Sqrt, [t_ps], [t_r], scale=1.0 / D, bias=EPS)
            fw.op("dve", lambda e: e.reciprocal(out=r[:], in_=r[:]), [t_r], [t_r])
            for kc in range(KC):
                fw.op("dve", lambda e: e.scalar_tensor_tensor(
                    out=h[:, kc, t0:t0 + TB], in0=xt[:, kc, :], scalar=prm_sb[:, l, C.p_n1 + kc:C.p_n1 + kc + 1],
                    in1=r[:], op0=ALU.mult, op1=ALU.mult), [t_x, t_r, t_prm], [t_h[tb]])
        wst = RingS("A_wst", 3, [128, KC, 128], F32)
        wbf = RingS("A_wbf", 3, [128, KC, 128], BF16)
        NH2 = 2 if NTB >= 2 else 1
        SH = S // NH2
        ostf = RingS("A_of", 2, [128, SH], F32)
        ostb = RingS("A_ob", 2, [128, SH], BF16)
        wv = w_in[l].rearrange("(c p) n -> p c n", p=128)
        segs = []
        for i in range(4 * BC):
            segs.append((C.o_gq + 128 * i, projT[r_gq + 128 * i:r_gq + 128 * (i + 1), :], False))
        for i in range(2 * BC):
            segs.append((C.o_sq + 128 * i, sqkT[128 * i:128 * (i + 1), :], True))
        for i in range(3 * BC):
            segs.append((C.o_scx + 128 * i, projT[r_sc + 128 * i:r_sc + 128 * (i + 1), :], False))
        for i in range(3 * KC):
            segs.append((C.o_gates + 128 * i, projT[r_gates + 128 * i:r_gates + 128 * (i + 1), :], False))
        nev = 0
        pend = []

        def issue_w(i_):
            ws_, t_ws_ = wst.next()
            fw.dma("sp", ws_[:], wv[:, :, segs[i_][0]:segs[i_][0] + 128], [], [t_ws_])
            pend.append((ws_, t_ws_))

        for i_ in range(min(2, len(segs))):
            issue_w(i_)
        for si, (wc, dst, isb) in enumerate(segs):
            ws, t_ws = pend.pop(0)
            if si + 2 < len(segs):
                issue_w(si + 2)
            wb, t_wb = wbf.next()
            fw.op("pool", lambda e: e.tensor_copy(out=wb[:], in_=ws[:]), [t_ws], [t_wb])
            for hf in range(NH2):
                ost, t_o = (ostb if isb else ostf).next()
                for tb in range(hf * NTB // NH2, (hf + 1) * NTB // NH2):
                    t0 = tb * TB
                    o0 = t0 - hf * SH
                    ps, t_ps = PS.next()
                    for kc in range(KC):
                        fw.op("pe", lambda e: e.matmul(ps[:, 0:TB], wb[:, kc, :], h[:, kc, t0:t0 + TB],
                                                       start=(kc == 0), stop=(kc == KC - 1)), [t_wb, t_h[tb]], [t_ps])
                    if nev % 2 == 0:
                        act(fw, ost[:, o0:o0 + TB], ps[:, 0:TB], AF.Copy, [t_ps], [t_o])
                    else:
                        fw.op("dve", lambda e: e.tensor_copy(out=ost[:, o0:o0 + TB], in_=ps[:, 0:TB]), [t_ps], [t_o])
                    nev += 1
                fw.dma("sp", dst[:, hf * SH:(hf + 1) * SH], ost[:], [t_o], [])
        wvs = sb("A_wvs", [128, KC, BW + 2 * HG], F32)
        wvb = sb("A_wvb", [128, KC, BW + 2 * HG], BF16)
        t_wv = Tok()
        fw.dma("sp", wvs[:, :, 0:BW], wv[:, :, C.o_sq + 2 * BW:C.o_sq + 3 * BW], [], [t_wv])
        fw.dma("sp", wvs[:, :, BW:BW + 2 * HG], wv[:, :, C.o_a:C.o_a + 2 * HG], [], [t_wv])
        t_wvb = Tok()
        fw.op("pool", lambda e: e.tensor_copy(out=wvb[:], in_=wvs[:]), [t_wv], [t_wvb])
        vst = RingS("A_vst", 2, [128, 4, BW], BF16)
        abst = sb("A_abst", [128, NT, 2 * HG], F32)
        t_ab = Tok()
        svv = sv_tm.rearrange("(t p) n -> p t n", p=128)
        for tg in range(NT // 4 if NT >= 4 else 1):
            ntile = min(4, NT)
            vs, t_vs = vst.next()
            for j in range(ntile):
                tt = tg * 4 + j
                tbi = (tt * 128) // TB
                ps, t_ps = PS.next()
                for kc in range(KC):
                    fw.op("pe", lambda e: e.matmul(ps[:, 0:BW], h[:, kc, tt * 128:(tt + 1) * 128], wvb[:, kc, 0:BW],
                                                   start=(kc == 0), stop=(kc == KC - 1)), [t_wvb, t_h[tbi]], [t_ps])
                act(fw, vs[:, j, :], ps[:, 0:BW], AF.Copy, [t_ps], [t_vs])
                ps2, t_ps2 = PS.next()
                for kc in range(KC):
                    fw.op("pe", lambda e: e.matmul(ps2[:, 0:2 * HG], h[:, kc, tt * 128:(tt + 1) * 128],
                                                   wvb[:, kc, BW:BW + 2 * HG],
                                                   start=(kc == 0), stop=(kc == KC - 1)), [t_wvb, t_h[tbi]], [t_ps2])
                fw.op("dve", lambda e: e.tensor_copy(out=abst[:, tt, :], in_=ps2[:, 0:2 * HG]), [t_ps2], [t_ab])
            fw.dma("sp", svv[:, tg * 4:tg * 4 + ntile, :], vs[:, 0:ntile, :], [t_vs], [])
        abv_ = ab_tm.rearrange("(t p) n -> p t n", p=128)
        for g_ in range(0, NT, 8):
            fw.dma("sp", abv_[:, g_:min(NT, g_ + 8), :], abst[:, g_:min(NT, g_ + 8), :], [t_ab], [])

    TQ = min(512, S)
    NQB = S // TQ
    MD = TQ // 128
    ybT = nc.dram_tensor("ybT", [BW, S], BF16, kind=okind).ap()
    ycT = nc.dram_tensor("ycT", [BW, S], BF16, kind=okind).ap()
    negL8 = nc.alloc_sbuf_tensor("negL8", [128, 128], BF16)
    ones8 = nc.alloc_sbuf_tensor("ones8", [128, 128], BF16)
    maskD = nc.alloc_sbuf_tensor("maskD", [128, MD, TQ], BF16)
    ctmp = nc.alloc_sbuf_tensor("ctmp", [128, TQ], BF16)
    fw.op("pool", lambda e: e.memset(ctmp[:], -8.0), [], [t_const])
    fw.op("pool", lambda e: e.affine_select(out=negL8[:], in_=ctmp[:, 0:128], pattern=[[-1, 128]],
                                            compare_op=ALU.is_ge, fill=0.0, base=0, channel_multiplier=1),
          [t_const], [t_const])
    fw.op("pool", lambda e: e.memset(ones8[:], 8.0), [], [t_const])
    fw.op("pool", lambda e: e.memset(ctmp[:], 1.0), [t_const], [t_const])
    for m in range(MD):
        fw.op("pool", lambda e: e.affine_select(out=maskD[:, m, :], in_=ctmp[:], pattern=[[1, TQ]],
                                                compare_op=ALU.is_gt, fill=0.0, base=-128 * m, channel_multiplier=-1),
              [t_const], [t_const])

    def phase_C(l):
        qk = sb("C_qk", [128, 2 * BC, S], BF16)
        vv = sb("C_v", [128, NT, BW], BF16)
        yb = sb("C_yb", [128, BC, S], BF16)
        t_qk, t_v, t_yb = Tok(), Tok(), Tok()
        for c_ in range(2 * BC):
            fw.dma("sp", qk[:, c_, :], sqkT[c_ * 128:(c_ + 1) * 128, :], [], [t_qk])
        svv_ = sv_tm.rearrange("(t p) n -> p t n", p=128)
        for g_ in range(0, NT, 4):
            fw.dma("sp", vv[:, g_:min(NT, g_ + 4), :], svv_[:, g_:min(NT, g_ + 4), :], [], [t_v])
        e_r = RingS("C_e", 2, [128, TQ], F32)
        sp_r = RingS("C_sp", 3, [128, TQ], BF16)
        ar_r = RingS("C_ar", 2, [128, TQ], F32)
        at_r = RingS("C_at", 3, [128, TQ], BF16)
        rs_r = RingS("C_rs", 2, [128, TQ], F32)
        for I in range(NQB):
            q0 = I * TQ
            nkb = (q0 + TQ) // 128
            for hh in range(HS):
                hp, hc = hh % 2, hh // 2
                pl, ph = hp * 64, hp * 64 + 64
                po, t_po = PSO.next()
                rsb, t_rs = rs_r.next()
                qT = qk[pl:ph, hc, q0:q0 + TQ]
                for j in range(nkb - 1, -1, -1):
                    diag = (128 * j >= q0)
                    m = (128 * j - q0) // 128
                    kT = qk[pl:ph, BC + hc, 128 * j:128 * (j + 1)]
                    pz, t_pz = PS.next()
                    fw.op("pe", lambda e: e.matmul(pz[:, 0:TQ], kT, qT, start=True, stop=True), [t_qk], [t_pz])
                    eb, t_e = e_r.next()
                    act(fw, eb[:], pz[:, 0:TQ], AF.Exp, [t_pz], [t_e], scale=0.125)
                    spb, t_sp = sp_r.next()
                    act(fw, spb[:], eb[:], AF.Ln, [t_e], [t_sp], bias=1.0)
                    if diag:
                        fw.op("pool", lambda e: e.tensor_tensor(out=spb[:], in0=spb[:], in1=maskD[:, m, :], op=ALU.mult),
                              [t_sp, t_const], [t_sp])
                    pa, t_pa = PS.next()
                    fw.op("pe", lambda e: e.matmul(pa[:, 0:TQ], kT, qT, start=True, stop=False), [t_qk], [t_pa])
                    fw.op("pe", lambda e: e.matmul(pa[:, 0:TQ], negL8[:], spb[:], start=False, stop=True),
                          [t_sp, t_const], [t_pa])
                    atb, t_at = at_r.next()
                    if j == nkb - 1:
                        act(fw, atb[:], pa[:, 0:TQ], AF.Exp, [t_pa], [t_at], scale=0.125)
                    else:
                        arb, t_ar = ar_r.next()
                        fw.op("dve", lambda e: e.tensor_tensor(out=arb[:], in0=pa[:, 0:TQ], in1=rsb[:], op=ALU.subtract),
                              [t_pa, t_rs], [t_ar])
                        act(fw, atb[:], arb[:], AF.Exp, [t_ar], [t_at], scale=0.125)
                    if diag:
                        fw.op("pool", lambda e: e.tensor_tensor(out=atb[:], in0=atb[:], in1=maskD[:, m, :], op=ALU.mult),
                              [t_at, t_const], [t_at])
                    fw.op("pe", lambda e: e.matmul(po[pl:ph, 0:TQ], vv[:, j, hh * 64:(hh + 1) * 64], atb[:],
                                                   start=(j == nkb - 1), stop=(j == 0)), [t_v, t_at], [t_po])
                    if j > 0:
                        pc, t_pc = PS.next()
                        fw.op("pe", lambda e: e.matmul(pc[:, 0:TQ], ones8[:], spb[:], start=True, stop=True),
                              [t_sp, t_const], [t_pc])
                        if j == nkb - 1:
                            fw.op("dve", lambda e: e.tensor_copy(out=rsb[:], in_=pc[:, 0:TQ]), [t_pc], [t_rs])
                        else:
                            fw.op("dve", lambda e: e.tensor_tensor(out=rsb[:], in0=pc[:, 0:TQ], in1=rsb[:], op=ALU.add),
                                  [t_pc, t_rs], [t_rs])
                act(fw, yb[pl:ph, hc, q0:q0 + TQ], po[pl:ph, 0:TQ], AF.Copy, [t_po], [t_yb])
        for c_ in range(BC):
            fw.dma("sp", ybT[c_ * 128:(c_ + 1) * 128, :], yb[:, c_, :], [t_yb], [])

    def phase_D(l):
        xr = RingS("D_x", 2, [128, TB + 2], F32)
        cr = RingS("D_c", 2, [128, TB + 2], F32)
        br = RingS("D_b", 2, [128, TB], F32)
        ar = RingS("D_a", 2, [128, TB], F32)
        yr = RingS("D_y", 2, [128, TB], BF16)
        for c in range(BC):
            rx, rb, rc = r_sc + 128 * c, r_sc + BW + 128 * c, r_sc + 2 * BW + 128 * c
            for tb in range(NTB):
                t0 = tb * TB
                xt, t_x = xr.next()
                ct, t_c = cr.next()
                bt, t_b = br.next()
                if tb == 0:
                    fw.op("dve", lambda e: e.memset(xt[:, 0:2], 0.0), [], [t_x])
                    fw.op("dve", lambda e: e.memset(ct[:, 0:2], 0.0), [], [t_c])
                    fw.dma("sp", xt[:, 2:TB + 2], projT[rx:rx + 128, 0:TB], [], [t_x])
                    fw.dma("sp", ct[:, 2:TB + 2], projT[rc:rc + 128, 0:TB], [], [t_c])
                else:
                    fw.dma("sp", xt[:], projT[rx:rx + 128, t0 - 2:t0 + TB], [], [t_x])
                    fw.dma("sp", ct[:], projT[rc:rc + 128, t0 - 2:t0 + TB], [], [t_c])
                fw.dma("sp", bt[:], projT[rb:rb + 128, t0:t0 + TB], [], [t_b])
                fw.op("dve", lambda e: e.tensor_tensor(out=xt[:], in0=xt[:], in1=ct[:], op=ALU.mult), [t_x, t_c], [t_x])
                at, t_a = ar.next()
                wcol = lambda tap: prm_sb[:, l, C.p_cs + tap * BC + c:C.p_cs + tap * BC + c + 1]
                fw.op("dve", lambda e: e.tensor_scalar(out=at[:], in0=xt[:, 0:TB], scalar1=wcol(0), scalar2=None,
                                                       op0=ALU.mult), [t_x, t_prm], [t_a])
                for tap in (1, 2):
                    fw.op("dve", lambda e: e.scalar_tensor_tensor(out=at[:], in0=xt[:, tap:tap + TB], scalar=wcol(tap),
                                                                  in1=at[:], op0=ALU.mult, op1=ALU.add),
                          [t_x, t_prm, t_a], [t_a])
                yt, t_y = yr.next()
                fw.op("dve", lambda e: e.tensor_tensor(out=yt[:], in0=at[:], in1=bt[:], op=ALU.mult), [t_a, t_b], [t_y])
                fw.dma("sp", ycT[128 * c:128 * (c + 1), t0:t0 + TB], yt[:], [t_y], [])

    yaT = nc.dram_tensor("yaT", [BW, S], BF16, kind=okind).ap()
    ident = nc.alloc_sbuf_tensor("ident", [128, 128], F32)
    bdiag = nc.alloc_sbuf_tensor("bdiag", [128, 128], F32)
    LOs = nc.alloc_sbuf_tensor("LOs", [128, 128], F32)
    LOi = nc.alloc_sbuf_tensor("LOi", [128, 128], F32)
    UPi = nc.alloc_sbuf_tensor("UPi", [128, 128], F32)
    sel0 = nc.alloc_sbuf_tensor("sel0", [128, 128], F32)
    sel1 = nc.alloc_sbuf_tensor("sel1", [128, 128], F32)
    fw.op("pool", lambda e: e.affine_select(out=ident[:], in_=ones_f[:], pattern=[[-1, 128]], compare_op=ALU.is_equal,
                                            fill=0.0, base=0, channel_multiplier=1), [t_const], [t_const])
    fw.op("pool", lambda e: e.memset(bdiag[:], 0.0), [], [t_const])
    fw.op("pool", lambda e: e.memset(bdiag[0:64, 0:64], 1.0), [t_const], [t_const])
    fw.op("pool", lambda e: e.memset(bdiag[64:128, 64:128], 1.0), [t_const], [t_const])
    fw.op("pool", lambda e: e.memset(sel0[:], 0.0), [], [t_const])
    fw.op("pool", lambda e: e.memset(sel0[0:64, :], 1.0), [t_const], [t_const])
    fw.op("pool", lambda e: e.memset(sel1[:], 0.0), [], [t_const])
    fw.op("pool", lambda e: e.memset(sel1[64:128, :], 1.0), [t_const], [t_const])
    fw.op("pool", lambda e: e.affine_select(out=LOs[:], in_=bdiag[:], pattern=[[-1, 128]], compare_op=ALU.is_gt,
                                            fill=0.0, base=0, channel_multiplier=1), [t_const], [t_const])
    fw.op("pool", lambda e: e.affine_select(out=LOi[:], in_=bdiag[:], pattern=[[-1, 128]], compare_op=ALU.is_ge,
                                            fill=0.0, base=0, channel_multiplier=1), [t_const], [t_const])
    fw.op("pool", lambda e: e.affine_select(out=UPi[:], in_=bdiag[:], pattern=[[1, 128]], compare_op=ALU.is_ge,
                                            fill=0.0, base=0, channel_multiplier=-1), [t_const], [t_const])

    def phase_B(l):
        HW = HG * 128
        P0 = C.p_cq
        ab = sb("B_ab", [128, NT, 2 * HG], F32)
        t_ab = Tok()
        abv_ = ab_tm.rearrange("(t p) n -> p t n", p=128)
        for g_ in range(0, NT, 8):
            fw.dma("sp", ab[:, g_:min(NT, g_ + 8), :], abv_[:, g_:min(NT, g_ + 8), :], [], [t_ab])
        beta = sb("B_beta", [128, NT, HG], F32)
        gg = sb("B_g", [128, NT, HG], F32)
        z1 = sb("B_z1", [128, NT, HG], F32)
        z2 = sb("B_z2", [128, NT, HG], F32)
        eG = sb("B_eG", [128, NT, HG], F32)
        kdf = sb("B_kdf", [128, NT, HG], F32)
        gl0 = sb("B_gl0", [128, NT, HG], F32)
        gl1 = sb("B_gl1", [128, NT, HG], F32)
        nexp = sb("B_nexp", [128, HG], F32)
        t_g = Tok()
        act(fw, beta[:], ab[:, :, HG:2 * HG], AF.Sigmoid, [t_ab], [t_g])
        for hh in range(HG):
            fw.op("dve", lambda e: e.tensor_scalar(out=gg[:, :, hh], in0=ab[:, :, hh],
                                                   scalar1=prm_sb[:, l, C.p_dt + hh:C.p_dt + hh + 1], scalar2=None,
                                                   op0=ALU.add), [t_ab, t_prm], [t_g])
        fw.op("dve", lambda e: e.tensor_scalar(out=z1[:], in0=gg[:], scalar1=0.0, scalar2=None, op0=ALU.max),
              [t_g], [t_g])
        fw.op("dve", lambda e: e.scalar_tensor_tensor(out=z2[:], in0=z1[:], scalar=-2.0, in1=gg[:], op0=ALU.mult,
                                                      op1=ALU.add), [t_g], [t_g])
        act(fw, z2[:], z2[:], AF.Exp, [t_g], [t_g])
        act(fw, z2[:], z2[:], AF.Ln, [t_g], [t_g], bias=1.0)
        fw.op("dve", lambda e: e.tensor_tensor(out=z2[:], in0=z2[:], in1=z1[:], op=ALU.add), [t_g], [t_g])
        act(fw, nexp[:], prm_sb[:, l, C.p_al:C.p_al + HG], AF.Exp, [t_prm], [t_g])
        for hh in range(HG):
            fw.op("dve", lambda e: e.tensor_scalar(out=gg[:, :, hh], in0=z2[:, :, hh], scalar1=nexp[:, hh:hh + 1],
                                                   scalar2=-1.0, op0=ALU.mult, op1=ALU.mult), [t_g], [t_g])
        NH = NT * HG
        ggf = gg[:].rearrange("p t h -> p (t h)")
        pg, t_pg = PS.next()
        fw.op("pe", lambda e: e.matmul(pg[:, 0:NH], UPi[:], ggf, start=True, stop=True), [t_g, t_const], [t_pg])
        act(fw, eG[:].rearrange("p t h -> p (t h)"), pg[:, 0:NH], AF.Exp, [t_pg], [t_g])
        pg2, t_pg2 = PS.next()
        fw.op("pe", lambda e: e.matmul(pg2[:, 0:NH], bdiag[:], ggf, start=True, stop=True), [t_g, t_const], [t_pg2])
        fw.op("dve", lambda e: e.tensor_copy(out=z1[:].rearrange("p t h -> p (t h)"), in_=pg[:, 0:NH]), [t_pg, t_g], [t_g])
        fw.op("dve", lambda e: e.tensor_tensor(out=kdf[:].rearrange("p t h -> p (t h)"), in0=pg2[:, 0:NH],
                                               in1=z1[:].rearrange("p t h -> p (t h)"), op=ALU.subtract),
              [t_pg2, t_g], [t_g])
        act(fw, kdf[:], kdf[:], AF.Exp, [t_g], [t_g])
        pg3, t_pg3 = PS.next()
        fw.op("pe", lambda e: e.matmul(pg3[:, 0:NH], sel0[:], ggf, start=True, stop=True), [t_g, t_const], [t_pg3])
        act(fw, gl0[:].rearrange("p t h -> p (t h)"), pg3[:, 0:NH], AF.Exp, [t_pg3], [t_g])
        pg4, t_pg4 = PS.next()
        fw.op("pe", lambda e: e.matmul(pg4[:, 0:NH], sel1[:], ggf, start=True, stop=True), [t_g, t_const], [t_pg4])
        act(fw, gl1[:].rearrange("p t h -> p (t h)"), pg4[:, 0:NH], AF.Exp, [t_pg4], [t_g])

        if B_STOP == 1:
            return
        Sst = sb("B_S", [128, HG, 128], F32)
        S16 = sb("B_S16", [128, HG, 128], BF16)
        t_S, t_S16 = Tok(), Tok()
        fw.op("dve", lambda e: e.memset(Sst[:], 0.0), [], [t_S])
        fw.op("dve", lambda e: e.memset(S16[:], 0.0), [], [t_S16])

        xr = RingS("B_x", 3, [128, TB + 3], F32)
        acr = RingS("B_acc", 2, [128, TB], F32)
        slr = RingS("B_sl", 2, [128, 3 * BC, TB], F32)
        gsr = RingS("B_gs", 2, [128, BC, TB], F32)
        yar = RingS("B_ya", 2, [128, BC, TB], BF16)
        wTr = RingS("B_wT", 2, [128, HG, 128], BF16)
        qdr = RingS("B_qd", 2, [128, HG, 128], BF16)
        aqr = RingS("B_aq", 2, [128, HG, 128], BF16)
        kdr = RingS("B_kd", 2, [128, HG, 128], BF16)
        uur = RingS("B_u", 2, [128, HG, 128], F32)
        def arr(name, shape, dt, n=1):
            return [(sb(name, shape, dt), [Tok() for _ in range(HG)]) for _ in range(n)]
        src4 = arr("B_src4", [128, HG, 4, 128], F32)[0]
        fm4 = arr("B_fm4", [128, HG, 4, 128], BF16)[0]
        vbb = arr("B_vb", [128, HG, 128], BF16)[0]
        kbg = arr("B_kbg", [128, HG, 128], BF16)[0]
        gsm = arr("B_gsm", [128, HG, 128], F32)[0]
        dm = arr("B_dm", [128, HG, 3, 128], F32)[0]
        aqf = arr("B_aqf", [128, HG, 128], F32)[0]
        ABs = arr("B_AB", [128, HG, 2, 128], F32, 2)
        TTs = arr("B_TT", [128, HG, 128], F32, 2)
        X16 = arr("B_X16", [128, HG, 4, 128], BF16, 2)
        T16 = arr("B_T16", [128, HG, 2, 128], BF16, 2)
        ssq = arr("B_ssq", [128, HG, 2], F32)[0]
        junk = sb("B_junk", [128, 128], BF16)
        t_junk = Tok()
        onr = RingS("B_on", 2, [128, HG, 128], F32)
        oss = RingS("B_oss", 2, [128, HG], F32)

        def load_block(tb):
            t0 = tb * TB
            sl, t_sl = slr.next()
            for ch in range(3 * BC):
                xt, t_x = xr.next()
                r0 = r_gq + 128 * ch
                if tb == 0:
                    fw.op("dve", lambda e: e.memset(xt[:, 0:3], 0.0), [], [t_x])
                    fw.dma("sp", xt[:, 3:TB + 3], projT[r0:r0 + 128, 0:TB], [], [t_x])
                else:
                    fw.dma("sp", xt[:], projT[r0:r0 + 128, t0 - 3:t0 + TB], [], [t_x])
                acc, t_a = acr.next()
                wcol = lambda tap: prm_sb[:, l, P0 + tap * 3 * BC + ch:P0 + tap * 3 * BC + ch + 1]
                fw.op("dve", lambda e: e.tensor_scalar(out=acc[:], in0=xt[:, 0:TB], scalar1=wcol(0), scalar2=None,
                                                       op0=ALU.mult), [t_x, t_prm], [t_a])
                for tap in (1, 2, 3):
                    fw.op("dve", lambda e: e.scalar_tensor_tensor(out=acc[:], in0=xt[:, tap:tap + TB], scalar=wcol(tap),
                                                                  in1=acc[:], op0=ALU.mult, op1=ALU.add),
                          [t_x, t_prm, t_a], [t_a])
                act(fw, sl[:, ch, :], acc[:], AF.Silu, [t_a], [t_sl])
            gs, t_gs = gsr.next()
            fw.dma("sp", gs[:], projT[r_gate:r_gate + BW, t0:t0 + TB].rearrange("(c p) s -> p c s", p=128), [], [t_gs])
            act(fw, gs[:], gs[:], AF.Silu, [t_gs], [t_gs])
            fw.op("pool", lambda e: e.tensor_scalar(out=gs[:], in0=gs[:], scalar1=prm_sb[:, l, C.p_gn:C.p_gn + 1],
                                                    scalar2=1.0, op0=ALU.mult, op1=ALU.mult), [t_gs, t_prm], [t_gs])
            return sl, t_sl, gs, t_gs

        def prep(tt, sl, t_sl, tc):
            wT, t_wT = wTr.next()
            qd, t_qd = qdr.next()
            aq, t_aq = aqr.next()
            kd, t_kd = kdr.next()
            uu, t_uu = uur.next()
            bcol = lambda hh: beta[:, tt, hh:hh + 1]
            for hh in range(HG):
                pt, t_pt = PS.next()
                for i, ch in enumerate((hh, BC + hh, 2 * BC + hh)):
                    fw.op("pe", lambda e: e.transpose(pt[:, i * 128:(i + 1) * 128], sl[:, ch, tc:tc + 128], ident[:]),
                          [t_sl, t_const], [t_pt])
                sq, t_sq = ssq[0][:, hh, :], ssq[1][hh]
                act(fw, junk[:], pt[:, 0:128], AF.Square, [t_pt], [t_junk, t_sq], accum_out=sq[:, 0:1])
                act(fw, junk[:], pt[:, 128:256], AF.Square, [t_pt], [t_junk, t_sq], accum_out=sq[:, 1:2])
                act(fw, sq, sq, AF.Sqrt, [t_sq], [t_sq], bias=EPS)
                fw.op("dve", lambda e: e.reciprocal(out=sq, in_=sq), [t_sq], [t_sq])
                s4, t_s4 = src4[0][:, hh], src4[1][hh]
                fw.op("dve", lambda e: e.tensor_scalar(out=s4[:, 0, :], in0=pt[:, 128:256], scalar1=sq[:, 1:2],
                                                       scalar2=None, op0=ALU.mult), [t_pt, t_sq], [t_s4])
                fw.op("dve", lambda e: e.tensor_scalar(out=s4[:, 2, :], in0=pt[:, 0:128], scalar1=sq[:, 0:1],
                                                       scalar2=float(128 ** -0.5), op0=ALU.mult, op1=ALU.mult),
                      [t_pt, t_sq], [t_s4])
                fw.op("dve", lambda e: e.tensor_scalar(out=vbb[0][:, hh, :], in0=pt[:, 256:384], scalar1=bcol(hh),
                                                       scalar2=None, op0=ALU.mult), [t_pt, t_g], [vbb[1][hh]])
                fw.op("pool", lambda e: e.tensor_scalar(out=s4[:, 1, :], in0=s4[:, 0, :], scalar1=bcol(hh),
                                                        scalar2=1.0, op0=ALU.mult, op1=ALU.mult), [t_s4, t_g], [t_s4])
                fw.op("pool", lambda e: e.tensor_scalar(out=s4[:, 3, :], in0=s4[:, 2, :], scalar1=eG[:, tt, hh:hh + 1],
                                                        scalar2=1.0, op0=ALU.mult, op1=ALU.mult), [t_s4, t_g], [t_s4])
                fw.op("pool", lambda e: e.tensor_scalar(out=kbg[0][:, hh, :], in0=s4[:, 1, :],
                                                        scalar1=eG[:, tt, hh:hh + 1], scalar2=1.0, op0=ALU.mult,
                                                        op1=ALU.mult), [t_s4, t_g], [kbg[1][hh]])
                fw.op("pool", lambda e: e.tensor_scalar(out=kd[:, hh, :], in0=s4[:, 0, :],
                                                        scalar1=kdf[:, tt, hh:hh + 1], scalar2=1.0, op0=ALU.mult,
                                                        op1=ALU.mult), [t_s4, t_g], [t_kd])
                fw.op("pool", lambda e: e.tensor_scalar(out=gsm[0][:, hh, :], in0=LOs[:], scalar1=gg[:, tt, hh:hh + 1],
                                                        scalar2=1.0, op0=ALU.mult, op1=ALU.mult),
                      [t_const, t_g], [gsm[1][hh]])
            if B_STOP == 31:
                return None
            for hh in range(HG):
                s4, t_s4 = src4[0][:, hh], src4[1][hh]
                p4, t_p4 = PS.next()
                for i in range(4):
                    fw.op("pe", lambda e: e.transpose(p4[:, i * 128:(i + 1) * 128], s4[:, i, :], ident[:]),
                          [t_s4, t_const], [t_p4])
                f4, t_f4 = fm4[0][:, hh], fm4[1][hh]
                act(fw, f4.rearrange("p a b -> p (a b)"), p4[:, 0:512], AF.Copy, [t_p4], [t_f4])
            if B_STOP == 32:
                return None
            for hh in range(HG):
                f4, t_f4 = fm4[0][:, hh], fm4[1][hh]
                p5, t_p5 = PS.next()
                fw.op("pe", lambda e: e.matmul(p5[:, 0:128], f4[:, 1, :], f4[:, 0, :], start=True, stop=True), [t_f4], [t_p5])
                fw.op("pe", lambda e: e.matmul(p5[:, 128:256], f4[:, 2, :], f4[:, 0, :], start=True, stop=True), [t_f4], [t_p5])
                fw.op("pe", lambda e: e.matmul(p5[:, 256:384], UPi[:], gsm[0][:, hh, :], start=True, stop=True),
                      [gsm[1][hh], t_const], [t_p5])
                d3, t_d3 = dm[0][:, hh], dm[1][hh]
                act(fw, d3[:, 0, :], p5[:, 256:384], AF.Exp, [t_p5], [t_d3])
                fw.op("pool", lambda e: e.tensor_tensor(out=d3[:, 1, :], in0=d3[:, 0, :], in1=LOs[:], op=ALU.mult),
                      [t_d3, t_const], [t_d3])
                fw.op("pool", lambda e: e.tensor_tensor(out=d3[:, 2, :], in0=d3[:, 0, :], in1=LOi[:], op=ALU.mult),
                      [t_d3, t_const], [t_d3])
                AB, t_AB = ABs[0][0][:, hh], ABs[0][1][hh]
                fw.op("dve", lambda e: e.tensor_tensor(out=AB[:, 0, :], in0=p5[:, 0:128], in1=d3[:, 1, :], op=ALU.mult),
                      [t_p5, t_d3], [t_AB])
                fw.op("dve", lambda e: e.tensor_tensor(out=aqf[0][:, hh, :], in0=p5[:, 128:256], in1=d3[:, 2, :],
                                                       op=ALU.mult), [t_p5, t_d3], [aqf[1][hh]])
            if B_STOP == 33:
                return None
            for hh in range(HG):
                AB, t_AB = ABs[0][0][:, hh], ABs[0][1][hh]
                p6, t_p6 = PS.next()
                fw.op("pe", lambda e: e.transpose(p6[:, 0:128], AB[:, 0, :], ident[:]), [t_AB, t_const], [t_p6])
                fw.op("pe", lambda e: e.transpose(p6[:, 128:256], aqf[0][:, hh, :], ident[:]), [aqf[1][hh], t_const], [t_p6])
                X0, t_X0 = X16[0][0][:, hh], X16[0][1][hh]
                act(fw, X0[:, 0, :], AB[:, 0, :], AF.Copy, [t_AB], [t_X0])
                fw.op("pool", lambda e: e.tensor_tensor(out=X0[:, 2, :], in0=AB[:, 0, :], in1=X0[:, 0, :], op=ALU.subtract),
                      [t_AB, t_X0], [t_X0])
                act(fw, X0[:, 1, :], p6[:, 0:128], AF.Copy, [t_p6], [t_X0])
                fw.op("dve", lambda e: e.tensor_tensor(out=X0[:, 3, :], in0=p6[:, 0:128], in1=X0[:, 1, :], op=ALU.subtract),
                      [t_p6, t_X0], [t_X0])
                TT0, t_TT0 = TTs[0][0][:, hh, :], TTs[0][1][hh]
                fw.op("dve", lambda e: e.tensor_tensor(out=TT0, in0=ident[:], in1=p6[:, 0:128], op=ALU.subtract),
                      [t_p6, t_const], [t_TT0])
                Th0, t_Th0 = T16[0][0][:, hh], T16[0][1][hh]
                act(fw, Th0[:, 0, :], TT0, AF.Copy, [t_TT0], [t_Th0])
                fw.op("pool", lambda e: e.tensor_tensor(out=Th0[:, 1, :], in0=TT0, in1=Th0[:, 0, :], op=ALU.subtract),
                      [t_TT0, t_Th0], [t_Th0])
                act(fw, aq[:, hh, :], p6[:, 128:256], AF.Copy, [t_p6], [t_aq])
            if B_STOP == 34:
                return None
            fin = [0] * HG
            for hh in range(HG):
                cur = 0
                for lev in range(1, 6):
                    nxt = 1 - cur
                    X, t_X = X16[cur][0][:, hh], X16[cur][1][hh]
                    X2, t_X2 = X16[nxt][0][:, hh], X16[nxt][1][hh]
                    p7, t_p7 = PD7
                    for (wi, mi, st_, sp_) in ((1, 0, True, False), (1, 2, False, False), (3, 0, False, True)):
                        fw.op("pe", lambda e: e.matmul(p7[:, 0:128], X[:, wi, :], X[:, mi, :], start=st_, stop=sp_), [t_X], [t_p7])
                    ncol = 128
                    if lev < 5:
                        for (wi, mi, st_, sp_) in ((0, 1, True, False), (0, 3, False, False), (2, 1, False, True)):
                            fw.op("pe", lambda e: e.matmul(p7[:, 128:256], X[:, wi, :], X[:, mi, :], start=st_, stop=sp_), [t_X], [t_p7])
                        ncol = 256
                    X2f = X2.rearrange("p a b -> p (a b)")
                    act(fw, X2f[:, 0:ncol], p7[:, 0:ncol], AF.Copy, [t_p7], [t_X2])
                    fw.op("dve", lambda e: e.tensor_tensor(out=X2f[:, 256:256 + ncol], in0=p7[:, 0:ncol], in1=X2f[:, 0:ncol],
                                                           op=ALU.subtract), [t_p7, t_X2], [t_X2])
                    T0, t_T0 = TTs[cur][0][:, hh, :], TTs[cur][1][hh]
                    T1, t_T1 = TTs[nxt][0][:, hh, :], TTs[nxt][1][hh]
                    Th, t_Th = T16[cur][0][:, hh], T16[cur][1][hh]
                    Th2, t_Th2 = T16[nxt][0][:, hh], T16[nxt][1][hh]
                    p8, t_p8 = PD8
                    for (wi, mi, st_, sp_) in ((0, 0, True, False), (0, 1, False, False), (2, 0, False, True)):
                        fw.op("pe", lambda e: e.matmul(p8[:, 0:128], X2[:, wi, :], Th[:, mi, :], start=st_, stop=sp_),
                              [t_X2, t_Th], [t_p8])
                    fw.op("dve", lambda e: e.tensor_tensor(out=T1, in0=p8[:, 0:128], in1=T0, op=ALU.add), [t_p8, t_T0], [t_T1])
                    act(fw, Th2[:, 0, :], T1, AF.Copy, [t_T1], [t_Th2])
                    if lev < 5:
                        fw.op("pool", lambda e: e.tensor_tensor(out=Th2[:, 1, :], in0=T1, in1=Th2[:, 0, :], op=ALU.subtract),
                              [t_T1, t_Th2], [t_Th2])
                    cur = nxt
                fin[hh] = cur
            cur = fin[0]
            if B_STOP == 35:
                return None
            for hh in range(HG):
                tb_, t_tb = T16[cur][0][:, hh, 0, :], T16[cur][1][hh]
                p9, t_p9 = PS.next()
                fw.op("pe", lambda e: e.matmul(p9[:, 0:128], tb_, vbb[0][:, hh, :], start=True, stop=True),
                      [t_tb, vbb[1][hh]], [t_p9])
                fw.op("pe", lambda e: e.matmul(p9[:, 128:256], kbg[0][:, hh, :], tb_, start=True, stop=True),
                      [t_tb, kbg[1][hh]], [t_p9])
                fw.op("dve", lambda e: e.tensor_copy(out=uu[:, hh, :], in_=p9[:, 0:128]), [t_p9], [t_uu])
                act(fw, wT[:, hh, :], p9[:, 128:256], AF.Copy, [t_p9], [t_wT])
                fw.op("pool", lambda e: e.tensor_copy(out=qd[:, hh, :], in_=fm4[0][:, hh, 3, :]), [fm4[1][hh]], [t_qd])
            return (wT, t_wT, qd, t_qd, aq, t_aq, kd, t_kd, uu, t_uu)

        vnr = RingS("B_vn", 2, [128, HG, 128], BF16)

        def scan(tt, ops, gs, t_gs, ya, t_ya, tc):
            (wT, t_wT, qd, t_qd, aq, t_aq, kd, t_kd, uu, t_uu) = ops
            po, t_po = PSO.next()
            vn, t_vn = vnr.next()
            for x in (0, 1):
                r0, r1 = 64 * x, 64 * x + 64
                glx = gl0 if x == 0 else gl1
                p1, t_p1 = PS.next()
                for hh in range(HG):
                    fw.op("pe", lambda e: e.matmul(p1[r0:r1, hh * 128:(hh + 1) * 128], wT[:, hh, r0:r1], S16[:, hh, :],
                                                   start=True, stop=True), [t_wT, t_S16], [t_p1])
                fw.op("dve", lambda e: e.tensor_tensor(out=vn[r0:r1].rearrange("p a b -> p (a b)"),
                                                       in0=uu[r0:r1].rearrange("p a b -> p (a b)"),
                                                       in1=p1[r0:r1, 0:HW], op=ALU.subtract), [t_uu, t_p1], [t_vn])
                for hh in range(HG):
                    fw.op("pe", lambda e: e.matmul(po[r0:r1, hh * 128:(hh + 1) * 128], qd[:, hh, r0:r1], S16[:, hh, :],
                                                   start=True, stop=False), [t_qd, t_S16], [t_po])
                    fw.op("pe", lambda e: e.matmul(po[r0:r1, hh * 128:(hh + 1) * 128], aq[r0:r1, hh, r0:r1],
                                                   vn[r0:r1, hh, :], start=False, stop=True), [t_aq, t_vn], [t_po])
                p3, t_p3 = PS.next()
                for hh in range(HG):
                    fw.op("pe", lambda e: e.matmul(p3[:, hh * 128:(hh + 1) * 128], kd[r0:r1, hh, :], vn[r0:r1, hh, :],
                                                   start=True, stop=True), [t_kd, t_vn], [t_p3])
                for hh in range(HG):
                    fw.op("dve", lambda e: e.scalar_tensor_tensor(out=Sst[:, hh, :], in0=Sst[:, hh, :],
                                                                  scalar=glx[:, tt, hh:hh + 1],
                                                                  in1=p3[:, hh * 128:(hh + 1) * 128],
                                                                  op0=ALU.mult, op1=ALU.add), [t_S, t_p3, t_g], [t_S])
                act(fw, S16[:], Sst[:], AF.Copy, [t_S], [t_S16])
            os_, t_os = oss.next()
            on, t_on = onr.next()
            for hh in range(HG):
                act(fw, junk[:], po[:, hh * 128:(hh + 1) * 128], AF.Square, [t_po], [t_junk, t_os],
                    accum_out=os_[:, hh:hh + 1])
            act(fw, os_[:], os_[:], AF.Sqrt, [t_os], [t_os], scale=1.0 / 128, bias=EPS)
            fw.op("dve", lambda e: e.reciprocal(out=os_[:], in_=os_[:]), [t_os], [t_os])
            for hh in range(HG):
                fw.op("dve", lambda e: e.tensor_scalar(out=on[:, hh, :], in0=po[:, hh * 128:(hh + 1) * 128],
                                                       scalar1=os_[:, hh:hh + 1], scalar2=None, op0=ALU.mult),
                      [t_po, t_os], [t_on])
            pT, t_pT = PS.next()
            for hh in range(HG):
                fw.op("pe", lambda e: e.transpose(pT[:, hh * 128:(hh + 1) * 128], on[:, hh, :], ident[:]),
                      [t_on, t_const], [t_pT])
            fw.op("dve", lambda e: e.tensor_tensor(out=ya[:, :, tc:tc + 128],
                                                   in0=pT[:, 0:HW].rearrange("p (a b) -> p a b", a=HG),
                                                   in1=gs[:, :, tc:tc + 128], op=ALU.mult), [t_pT, t_gs], [t_ya])

        TPB = TB // 128
        blk = load_block(0)
        if B_STOP == 2:
            return
        pending = prep(0, blk[0], blk[1], 0)
        if B_STOP >= 3 and B_STOP != 4:
            return
        ya, t_ya = yar.next()
        for tt in range(NT):
            tb = tt // TPB
            cur_blk = blk
            cur_ops = pending
            if tt + 1 < NT:
                if (tt + 1) % TPB == 0:
                    blk = load_block(tb + 1)
                pending = prep(tt + 1, blk[0], blk[1], ((tt + 1) % TPB) * 128)
            scan(tt, cur_ops, cur_blk[2], cur_blk[3], ya, t_ya, (tt % TPB) * 128)
            if B_STOP == 4:
                return
            if (tt + 1) % TPB == 0:
                fw.dma("sp", yaT[:, tb * TB:(tb + 1) * TB].rearrange("(c p) s -> p c s", p=128), ya[:], [t_ya], [])
                ya, t_ya = yar.next()

    def rms_rstd(src, t_src, sq_ring, rs_ring):
        ps, t_ps = PS.next()
        for kc in range(KC):
            sq, t_sq = sq_ring.next()
            act(fw, sq[:], src[:, kc, :], AF.Square, [t_src], [t_sq])
            fw.op("pe", lambda e: e.matmul(ps[:, 0:TB], ones_f[:], sq[:], start=(kc == 0), stop=(kc == KC - 1)),
                  [t_sq, t_const], [t_ps])
        r, t_r = rs_ring.next()
        act(fw, r[:], ps[:, 0:TB], AF.Sqrt, [t_ps], [t_r], scale=1.0 / D, bias=EPS)
        fw.op("dve", lambda e: e.reciprocal(out=r[:], in_=r[:]), [t_r], [t_r])
        return r, t_r

    xmid = nc.dram_tensor("xmid", [D, S], F32, kind=okind).ap()
    xa = nc.dram_tensor("xa", [D, S], F32, kind=okind).ap()

    def phase_E(l, xsrc):
        wbb = sb("E_wbb", [128, 3, BC, D], BF16)
        wob = sb("E_wob", [128, KC, D], BF16)
        t_wb, t_wo = Tok(), Tok()
        wst = RingS("E_wst", 3, [128, D], F32)
        for i in range(3):
            for kc in range(BC):
                ws, t_ws = wst.next()
                fw.dma("sp", ws[:], w_br[l, i, kc * 128:(kc + 1) * 128, :], [], [t_ws])
                fw.op("pool", lambda e: e.tensor_copy(out=wbb[:, i, kc, :], in_=ws[:]), [t_ws], [t_wb])
        for kc in range(KC):
            ws, t_ws = wst.next()
            fw.dma("sp", ws[:], w_out[l, kc * 128:(kc + 1) * 128, :], [], [t_ws])
            fw.op("pool", lambda e: e.tensor_copy(out=wob[:, kc, :], in_=ws[:]), [t_ws], [t_wo])
        yr = RingS("E_y", 2, [128, 3, BC, TB], BF16)
        gr = RingS("E_g", 4, [128, TB], F32)
        tr = RingS("E_t", 3, [128, TB], F32)
        mgr = RingS("E_mg", 2, [128, KC, TB], BF16)
        m2r = RingS("E_m2", 1, [128, KC, TB], F32)
        xr = RingS("E_x", 2, [128, KC, TB], F32)
        sqr = RingS("E_sq", 2, [128, TB], F32)
        rsr = RingS("E_rs", 2, [128, TB], F32)
        ysrc = (yaT, ybT, ycT)
        xv = xsrc.rearrange("(c p) s -> p c s", p=128)
        xo = xmid.rearrange("(c p) s -> p c s", p=128)
        nev = 0
        for tb in range(NTB):
            t0 = tb * TB
            y, t_y = yr.next()
            for i in range(3):
                fw.dma("sp", y[:, i], ysrc[i][:, t0:t0 + TB].rearrange("(c p) s -> p c s", p=128), [], [t_y])
            xt, t_x = xr.next()
            fw.dma("sp", xt[:], xv[:, :, t0:t0 + TB], [], [t_x])
            mg, t_mg = mgr.next()
            for oc in range(KC):
                acc, t_acc = tr.next()
                for i in range(3):
                    g, t_gt = gr.next()
                    rg = r_gates + i * D + oc * 128
                    fw.dma("sp", g[:], projT[rg:rg + 128, t0:t0 + TB], [], [t_gt])
                    act(fw, g[:], g[:], AF.Sigmoid, [t_gt], [t_gt])
                    ps, t_ps = PS.next()
                    for kc in range(BC):
                        fw.op("pe", lambda e: e.matmul(ps[:, 0:TB], wbb[:, i, kc, oc * 128:(oc + 1) * 128], y[:, i, kc, :],
                                                       start=(kc == 0), stop=(kc == BC - 1)), [t_wb, t_y], [t_ps])
                    if i == 0:
                        fw.op("dve", lambda e: e.tensor_tensor(out=acc[:], in0=ps[:, 0:TB], in1=g[:], op=ALU.mult),
                              [t_ps, t_gt], [t_acc])
                    else:
                        fw.op("dve", lambda e: e.tensor_tensor(out=g[:], in0=ps[:, 0:TB], in1=g[:], op=ALU.mult),
                              [t_ps, t_gt], [t_gt])
                        if i == 1:
                            fw.op("pool", lambda e: e.tensor_tensor(out=acc[:], in0=acc[:], in1=g[:], op=ALU.add),
                                  [t_acc, t_gt], [t_acc])
                        else:
                            fw.op("pool", lambda e: e.tensor_tensor(out=mg[:, oc, :], in0=acc[:], in1=g[:], op=ALU.add),
                                  [t_acc, t_gt], [t_mg])
            m2, t_m2 = m2r.next()
            for oc in range(KC):
                ps, t_ps = PS.next()
                for kc in range(KC):
                    fw.op("pe", lambda e: e.matmul(ps[:, 0:TB], wob[:, kc, oc * 128:(oc + 1) * 128], mg[:, kc, :],
                                                   start=(kc == 0), stop=(kc == KC - 1)), [t_wo, t_mg], [t_ps])
                if nev % 2 == 0:
                    act(fw, m2[:, oc, :], ps[:, 0:TB], AF.Copy, [t_ps], [t_m2])
                else:
                    fw.op("dve", lambda e: e.tensor_copy(out=m2[:, oc, :], in_=ps[:, 0:TB]), [t_ps], [t_m2])
                nev += 1
            r, t_r = rms_rstd(m2, t_m2, sqr, rsr)
            for kc in range(KC):
                fw.op("dve", lambda e: e.scalar_tensor_tensor(out=m2[:, kc, :], in0=m2[:, kc, :],
                                                              scalar=prm_sb[:, l, C.p_n2 + kc:C.p_n2 + kc + 1], in1=r[:],
                                                              op0=ALU.mult, op1=ALU.mult), [t_m2, t_r, t_prm], [t_m2])
            fw.op("pool", lambda e: e.tensor_tensor(out=xt[:], in0=xt[:], in1=m2[:], op=ALU.add), [t_x, t_m2], [t_x])
            fw.dma("sp", xo[:, :, t0:t0 + TB], xt[:], [t_x], [])

    def phase_F(l, xdst):
        FC = C.FC
        xr = RingS("F_x", 2, [128, KC, TB], F32)
        h2r = RingS("F_h2", 1, [128, KC, TB], BF16)
        ur = RingS("F_u", 1, [128, FC, TB], BF16)
        w1s = RingS("F_w1s", 2, [128, KC, 128], F32)
        w1b = RingS("F_w1b", 2, [128, KC, 128], BF16)
        w2s = RingS("F_w2s", 2, [128, FC, 128], F32)
        w2b = RingS("F_w2b", 2, [128, FC, 128], BF16)
        rr = RingS("F_r", 2, [128, TB], F32)
        m2r = RingS("F_m2", 1, [128, KC, TB], F32)
        sqr = RingS("F_sq", 2, [128, TB], F32)
        rsr = RingS("F_rs", 2, [128, TB], F32)
        xv = xmid.rearrange("(c p) s -> p c s", p=128)
        xo = xdst.rearrange("(c p) s -> p c s", p=128)
        w1v = w_ff1[l].rearrange("(c p) n -> p c n", p=128)
        w2v = w_ff2[l].rearrange("(c p) n -> p c n", p=128)
        nev = 0
        for tb in range(NTB):
            t0 = tb * TB
            xt, t_x = xr.next()
            fw.dma("sp", xt[:], xv[:, :, t0:t0 + TB], [], [t_x])
            r, t_r = rms_rstd(xt, t_x, sqr, rsr)
            h2, t_h2 = h2r.next()
            for kc in range(KC):
                fw.op("dve", lambda e: e.scalar_tensor_tensor(out=h2[:, kc, :], in0=xt[:, kc, :],
                                                              scalar=prm_sb[:, l, C.p_n3 + kc:C.p_n3 + kc + 1], in1=r[:],
                                                              op0=ALU.mult, op1=ALU.mult), [t_x, t_r, t_prm], [t_h2])
            u, t_u = ur.next()
            for fc in range(FC):
                ws, t_ws = w1s.next()
                fw.dma("sp", ws[:], w1v[:, :, fc * 128:(fc + 1) * 128], [], [t_ws])
                wb, t_wb = w1b.next()
                fw.op("pool", lambda e: e.tensor_copy(out=wb[:], in_=ws[:]), [t_ws], [t_wb])
                ps, t_ps = PS.next()
                for kc in range(KC):
                    fw.op("pe", lambda e: e.matmul(ps[:, 0:TB], wb[:, kc, :], h2[:, kc, :], start=(kc == 0),
                                                   stop=(kc == KC - 1)), [t_wb, t_h2], [t_ps])
                rl, t_rl = rr.next()
                act(fw, rl[:], ps[:, 0:TB], AF.Relu, [t_ps], [t_rl])
                fw.op("pool", lambda e: e.tensor_tensor(out=u[:, fc, :], in0=rl[:], in1=rl[:], op=ALU.mult), [t_rl], [t_u])
            m2, t_m2 = m2r.next()
            for oc in range(KC):
                ws, t_ws = w2s.next()
                fw.dma("sp", ws[:], w2v[:, :, oc * 128:(oc + 1) * 128], [], [t_ws])
                wb, t_wb = w2b.next()
                fw.op("pool", lambda e: e.tensor_copy(out=wb[:], in_=ws[:]), [t_ws], [t_wb])
                ps, t_ps = PS.next()
                for fc in range(FC):
                    fw.op("pe", lambda e: e.matmul(ps[:, 0:TB], wb[:, fc, :], u[:, fc, :], start=(fc == 0),
                                                   stop=(fc == FC - 1)), [t_wb, t_u], [t_ps])
                if nev % 2 == 0:
                    act(fw, m2[:, oc, :], ps[:, 0:TB], AF.Copy, [t_ps], [t_m2])
                else:
                    fw.op("dve", lambda e: e.tensor_copy(out=m2[:, oc, :], in_=ps[:, 0:TB]), [t_ps], [t_m2])
                nev += 1
            r2, t_r2 = rms_rstd(m2, t_m2, sqr, rsr)
            for kc in range(KC):
                fw.op("dve", lambda e: e.scalar_tensor_tensor(out=m2[:, kc, :], in0=m2[:, kc, :],
                                                              scalar=prm_sb[:, l, C.p_n4 + kc:C.p_n4 + kc + 1], in1=r2[:],
                                                              op0=ALU.mult, op1=ALU.mult), [t_m2, t_r2, t_prm], [t_m2])
            fw.op("pool", lambda e: e.tensor_tensor(out=xt[:], in0=xt[:], in1=m2[:], op=ALU.add), [t_x, t_m2], [t_x])
            fw.dma("sp", xo[:, :, t0:t0 + TB], xt[:], [t_x], [])


    def run_phase(fn, *a):
        with contextlib.ExitStack() as es:
            st["es"] = es
            fn(*a)
            fw.barrier()
        st["es"] = None

    for l in range(C.NL):
        xsrc = xin if l == 0 else xa
        xdst = yout if l == C.NL - 1 else xa
        if "A" in phases:
            run_phase(phase_A, l, xsrc)
        if "B" in phases:
            run_phase(phase_B, l)
        if "C" in phases:
            run_phase(phase_C, l)
        if "D" in phases:
            run_phase(phase_D, l)
        if "E" in phases:
            run_phase(phase_E, l, xsrc)
        if "F" in phases:
            run_phase(phase_F, l, xdst)
    fw.barrier(("sp",))
    print("instructions", fw.nins, "waits", fw.nwait, "rot", fw.nrot)
    return nc


def pack_params(C, inp):
    NL = C.NL
    out = np.zeros((NL, 128, C.NP), np.float32)
    for l in range(NL):
        def fm(v):
            return np.ascontiguousarray(v.reshape(-1, 128).T)
        out[l, :, C.p_n1:C.p_n1 + C.KC] = fm(inp["norm_mix_pre"][l])
        out[l, :, C.p_n2:C.p_n2 + C.KC] = fm(inp["norm_mix_post"][l])
        out[l, :, C.p_n3:C.p_n3 + C.KC] = fm(inp["norm_ffn_pre"][l])
        out[l, :, C.p_n4:C.p_n4 + C.KC] = fm(inp["norm_ffn_post"][l])
        cq = inp["conv_qkv_w"][l]
        out[l, :, C.p_cq:C.p_cq + 12 * C.BC] = cq.reshape(4, 3 * C.BC, 128).transpose(2, 0, 1).reshape(128, -1)
        cs = inp["conv_sc_w"][l]
        out[l, :, C.p_cs:C.p_cs + 3 * C.BC] = cs.reshape(3, C.BC, 128).transpose(2, 0, 1).reshape(128, -1)
        out[l, :, C.p_gn] = inp["gdn_norm_w"][l]
        out[l, :, C.p_al:C.p_al + C.HG] = np.broadcast_to(inp["gdn_a_log"][l][None, :], (128, C.HG))
        out[l, :, C.p_dt:C.p_dt + C.HG] = np.broadcast_to(inp["gdn_dt_bias"][l][None, :], (128, C.HG))
    return out


def make_in_maps(C, inp, nb):
    prm = pack_params(C, inp)
    maps = []
    for b in range(nb):
        maps.append({
            "xT": np.ascontiguousarray(np.asarray(inp["x"][b]).T),
            "w_in": np.asarray(inp["w_in"]), "w_branch": np.asarray(inp["w_branch"]),
            "w_out": np.asarray(inp["w_out"]), "w_ff1": np.asarray(inp["w_ff1"]),
            "w_ff2": np.asarray(inp["w_ff2"]), "prm": prm,
        })
    return maps


def kernel(**inputs):
    C = Cfg()
    inp = {k: np.asarray(v) for k, v in inputs.items()}
    nc = build(C, phases=("A", "B", "C", "D", "E", "F"))
    maps = make_in_maps(C, inp, 8)
    res = run_bass_kernel_spmd(nc, maps, core_ids=list(range(8)))
    out = np.stack([np.ascontiguousarray(r["yT"].T) for r in res.results], axis=0)
    return out.astype(np.float32)
```
